# Optimizing a Trainium2 kernel written in Bass

```python
import jax
import jax.numpy as jnp
from jax import lax
import numpy as np

D_MODEL = 1024
BATCH = 8
SEQ = 2048
DEPTH = 4
DEC_BATCH = 128
DEC_SEQ = 4
PAST_LEN = 8192
PAGE_SIZE = 128

HEAD_DIM = 64
N_HEADS_A = 8
N_KV_A = 2
WINDOW_A = 128
B_PAIRS = ((128, 1), (512, 4), (2048, 16))
N_GROUPS_B = 3
HEADS_PER_GROUP_B = 4
POOL_WINDOWS = (2, 4, 8, 16)
POOL_GROUP_WIDTH = 64
POOL_STATE = 15

WIDTH_A = N_HEADS_A * HEAD_DIM
WIDTH_B = HEADS_PER_GROUP_B * HEAD_DIM
WIDTH_C = len(POOL_WINDOWS) * POOL_GROUP_WIDTH
PROJ_SIZES = (WIDTH_A, N_KV_A * HEAD_DIM, N_KV_A * HEAD_DIM, WIDTH_A,
              N_GROUPS_B * WIDTH_B, N_GROUPS_B * WIDTH_B, N_GROUPS_B * WIDTH_B, WIDTH_B,
              WIDTH_C, WIDTH_C, 3 * D_MODEL)
D_IN = 2 * WIDTH_A + 2 * N_KV_A * HEAD_DIM + 3 * N_GROUPS_B * WIDTH_B + WIDTH_B + 2 * WIDTH_C + 3 * D_MODEL
BLOCK = 128
RMS_EPS = 1e-6
NEG_INF = -1e30

kernel_name = 'hybrid_gated_swa_dilated_pool_step'


def _rmsnorm(x, w):
    xf = x.astype(jnp.float32)
    y = xf * lax.rsqrt(jnp.mean(xf * xf, axis=-1, keepdims=True) + RMS_EPS)
    return (y * w.astype(jnp.float32)).astype(x.dtype)


def _alibi_slopes(n):
    return 2.0 ** (-8.0 * (jnp.arange(n, dtype=jnp.float32) + 1.0) / n)


def _probs_lse(s, sink):
    m = jnp.max(s, axis=-1, keepdims=True)
    if sink is not None:
        sk = sink.astype(jnp.float32)[..., None, None]
        m = jnp.maximum(m, sk)
    e = jnp.exp(s - m)
    z = jnp.sum(e, axis=-1, keepdims=True)
    if sink is not None:
        z = z + jnp.exp(sk - m)
    return e / z, (m + jnp.log(z))[..., 0]


def _band_attention(q, k, v, slopes, nwin, dist_scale, sink):
    n, L, G, R, hd = q.shape
    nb = -(-L // BLOCK)
    pad = nb * BLOCK - L
    qb = jnp.pad(q, ((0, 0), (0, pad), (0, 0), (0, 0), (0, 0))).reshape(n, nb, BLOCK, G, R, hd)

    def windows(t):
        tp = jnp.pad(t, ((0, 0), (BLOCK, pad), (0, 0), (0, 0))).reshape(n, nb + 1, BLOCK, G, hd)
        return jnp.concatenate([tp[:, :-1], tp[:, 1:]], axis=2)

    kw, vw = windows(k), windows(v)
    s = jnp.einsum('nbqgrd,nbkgd->nbgrqk', qb, kw).astype(jnp.float32) * (hd ** -0.5)
    qi = jnp.arange(BLOCK)[:, None]
    ki = jnp.arange(2 * BLOCK)[None, :]
    dist = BLOCK + qi - ki
    kpos = jnp.arange(nb)[:, None, None] * BLOCK - BLOCK + ki[None]
    valid = (dist >= 0) & (dist <= nwin) & (kpos >= 0)
    bias = -(slopes * dist_scale)[:, :, None, None] * dist.astype(jnp.float32)
    s = jnp.where(valid[:, None, None], s + bias, NEG_INF)
    p, lse = _probs_lse(s, sink)
    o = jnp.einsum('nbgrqk,nbkgd->nbqgrd', p.astype(v.dtype), vw).reshape(n, nb * BLOCK, G, R, hd)[:, :L]
    lse = lse.transpose(0, 1, 4, 2, 3).reshape(n, nb * BLOCK, G, R)[:, :L]
    return o, lse


def _dilated_prompt(q, k, v, slopes, window, dilation, sink):
    n, t_len = q.shape[:2]
    sub = t_len // dilation

    def split(t):
        return t.reshape(n, sub, dilation, *t.shape[2:]).swapaxes(1, 2).reshape(n * dilation, sub, *t.shape[2:])

    def merge(t):
        return t.reshape(n, dilation, sub, *t.shape[2:]).swapaxes(1, 2).reshape(n, t_len, *t.shape[2:])

    o, lse = _band_attention(split(q), split(k), split(v), slopes, window // dilation, dilation, sink)
    return merge(o), merge(lse)


def _dilated_sample(q, k_ext, v_ext, slopes, window, dilation, sink):
    n, n_q, G, R, hd = q.shape
    n_prev = k_ext.shape[1] - n_q
    step = jnp.arange(window // dilation + 1)
    idx = n_prev + jnp.arange(n_q)[:, None] - dilation * step[None, :]
    valid = (PAST_LEN - n_prev + idx) >= 0
    kg = k_ext[:, idx]
    vg = v_ext[:, idx]
    s = jnp.einsum('nqgrd,nqkgd->ngrqk', q, kg).astype(jnp.float32) * (hd ** -0.5)
    bias = -(slopes * dilation)[:, :, None, None] * step.astype(jnp.float32)
    s = jnp.where(valid, s + bias, NEG_INF)
    p, lse = _probs_lse(s, sink)
    o = jnp.einsum('ngrqk,nqkgd->nqgrd', p.astype(v_ext.dtype), vg)
    return o, lse.transpose(0, 3, 1, 2)


def _last_rows(t, n_rows, axis):
    size = t.shape[axis]
    if size < n_rows:
        pads = [(0, 0)] * t.ndim
        pads[axis] = (n_rows - size, 0)
        t = jnp.pad(t, pads)
    return lax.slice_in_dim(t, t.shape[axis] - n_rows, t.shape[axis], axis=axis)


def _pool_mix(u_ext, n_prev, pos0, w_pool, pool_scale):
    n, L, C = u_ext.shape
    t_len = L - n_prev
    win = jnp.repeat(jnp.array(POOL_WINDOWS, dtype=jnp.int32), POOL_GROUP_WIDTH)
    uf = u_ext.astype(jnp.float32)
    csum = jnp.concatenate([jnp.zeros((n, 1, C), jnp.float32), jnp.cumsum(uf, axis=1)], axis=1)
    hi = n_prev + 1 + jnp.arange(t_len)
    lo = jnp.maximum(hi[:, None] - win[None, :], 0)
    wsum = csum[:, hi] - jnp.take_along_axis(csum, jnp.broadcast_to(lo[None], (n, t_len, C)), axis=1)
    cnt = jnp.minimum(pos0 + 1 + jnp.arange(t_len)[:, None], win[None, :]).astype(jnp.float32)
    diff = (wsum / cnt - uf[:, n_prev:]).reshape(n, t_len, len(POOL_WINDOWS), POOL_GROUP_WIDTH)
    y = jnp.einsum('ntgc,gce->ntge', diff, w_pool.astype(jnp.float32)).reshape(n, t_len, C)
    return (y * pool_scale.astype(jnp.float32)).astype(u_ext.dtype)


def _layer(x, norm_w, w_in, sink_a, w_proj_a, w_proj_b, w_proj_c, w_pool, pool_scale, w_out, past):
    n, t_len, _ = x.shape
    z = _rmsnorm(x, norm_w) @ w_in
    split_at = np.cumsum(PROJ_SIZES)[:-1].tolist()
    qa, ka, va, ga, qb, kb, vb, gb, uc, gc, mg = jnp.split(z, split_at, axis=-1)
    rep_a = N_HEADS_A // N_KV_A
    qa = qa.reshape(n, t_len, N_KV_A, rep_a, HEAD_DIM)
    kva = jnp.stack([ka.reshape(n, t_len, N_KV_A, HEAD_DIM), va.reshape(n, t_len, N_KV_A, HEAD_DIM)], axis=1)
    qb = qb.reshape(n, t_len, N_GROUPS_B, HEADS_PER_GROUP_B, 1, HEAD_DIM)
    kvb = jnp.stack([kb.reshape(n, t_len, N_GROUPS_B, HEADS_PER_GROUP_B, HEAD_DIM),
                     vb.reshape(n, t_len, N_GROUPS_B, HEADS_PER_GROUP_B, HEAD_DIM)], axis=1)
    slopes_a = _alibi_slopes(N_HEADS_A).reshape(N_KV_A, rep_a)
    slopes_b = _alibi_slopes(N_GROUPS_B * HEADS_PER_GROUP_B).reshape(N_GROUPS_B, HEADS_PER_GROUP_B, 1)

    def extend(new_rows, i, axis):
        return new_rows if past is None else jnp.concatenate([past[i], new_rows], axis=axis)

    def attend(q, kv_ext, slopes, window, dilation, sink):
        fn = _dilated_prompt if past is None else _dilated_sample
        return fn(q, kv_ext[:, 0], kv_ext[:, 1], slopes, window, dilation, sink)

    kva_ext = extend(kva, 0, 2)
    oa, _ = attend(qa, kva_ext, slopes_a, WINDOW_A, 1, sink_a.reshape(N_KV_A, rep_a))
    new_a = _last_rows(kva_ext, WINDOW_A, 2)

    outs, lses, new_b = [], [], []
    for g, (win, dil) in enumerate(B_PAIRS):
        kv_ext = extend(kvb[:, :, :, g], 1 + g, 2)
        o_g, lse_g = attend(qb[:, :, g], kv_ext, slopes_b[g], win, dil, None)
        outs.append(o_g.astype(jnp.float32))
        lses.append(lse_g)
        new_b.append(_last_rows(kv_ext, win, 2))
    w_mix = jax.nn.softmax(jnp.stack(lses), axis=0)
    ob = jnp.sum(w_mix[..., None] * jnp.stack(outs), axis=0)

    u_ext = extend(uc, 4, 1)
    n_prev = 0 if past is None else POOL_STATE
    pos0 = 0 if past is None else PAST_LEN
    oc = _pool_mix(u_ext, n_prev, pos0, w_pool, pool_scale)
    new_c = _last_rows(u_ext, POOL_STATE, 1)

    ma, mb, mc = jnp.split(mg, 3, axis=-1)
    pa = (oa.reshape(n, t_len, WIDTH_A).astype(x.dtype) * jax.nn.silu(ga)) @ w_proj_a
    pb = (ob.reshape(n, t_len, WIDTH_B).astype(x.dtype) * jax.nn.silu(gb)) @ w_proj_b
    pc = (oc * jax.nn.silu(gc)) @ w_proj_c
    m = jax.nn.sigmoid(ma) * pa + jax.nn.sigmoid(mb) * pb + jax.nn.sigmoid(mc) * pc
    return x + m @ w_out, (new_a, new_b[0], new_b[1], new_b[2], new_c)


def _stack_layers(states, i):
    return jnp.stack([s[i] for s in states], axis=0)


def setup_inputs(seed: int = 0) -> dict:
    key = jax.random.key(seed)
    ks = jax.random.split(key, 17)

    def nrm(k, shape, scale):
        return jax.random.normal(k, shape, jnp.float32) * scale

    kv_b = [(DEPTH, DEC_BATCH, 2, w, HEADS_PER_GROUP_B, HEAD_DIM) for (w, _) in B_PAIRS]
    return {
        'x_prompt': nrm(ks[0], (BATCH, SEQ, D_MODEL), 1.0),
        'x_sample': nrm(ks[1], (DEC_BATCH, DEC_SEQ, D_MODEL), 1.0),
        'state_a': nrm(ks[2], (DEPTH, DEC_BATCH, 2, WINDOW_A, N_KV_A, HEAD_DIM), 1.0),
        'state_b1': nrm(ks[3], kv_b[0], 1.0),
        'state_b2': nrm(ks[4], kv_b[1], 1.0),
        'state_b3': nrm(ks[5], kv_b[2], 1.0),
        'state_c': nrm(ks[6], (DEPTH, DEC_BATCH, POOL_STATE, WIDTH_C), 1.0),
        'norm_w': 1.0 + nrm(ks[7], (DEPTH, D_MODEL), 0.1),
        'final_norm_w': 1.0 + nrm(ks[8], (D_MODEL,), 0.1),
        'w_in': nrm(ks[9], (DEPTH, D_MODEL, D_IN), D_MODEL ** -0.5),
        'sink_a': nrm(ks[10], (DEPTH, N_HEADS_A), 0.5),
        'w_proj_a': nrm(ks[11], (DEPTH, WIDTH_A, D_MODEL), WIDTH_A ** -0.5),
        'w_proj_b': nrm(ks[12], (DEPTH, WIDTH_B, D_MODEL), WIDTH_B ** -0.5),
        'w_proj_c': nrm(ks[13], (DEPTH, WIDTH_C, D_MODEL), WIDTH_C ** -0.5),
        'w_pool': nrm(ks[14], (DEPTH, len(POOL_WINDOWS), POOL_GROUP_WIDTH, POOL_GROUP_WIDTH), POOL_GROUP_WIDTH ** -0.5),
        'pool_scale': 1.0 + nrm(ks[15], (DEPTH, WIDTH_C), 0.1),
        'w_out': nrm(ks[16], (DEPTH, D_MODEL, D_MODEL), 0.5 * D_MODEL ** -0.5),
    }


def reference(x_prompt, x_sample, state_a, state_b1, state_b2, state_b3, state_c,
              norm_w, final_norm_w, w_in, sink_a, w_proj_a, w_proj_b, w_proj_c,
              w_pool, pool_scale, w_out):
    hp, hs = x_prompt, x_sample
    new_p, new_s = [], []
    for l in range(DEPTH):
        lw = (norm_w[l], w_in[l], sink_a[l], w_proj_a[l], w_proj_b[l], w_proj_c[l],
              w_pool[l], pool_scale[l], w_out[l])
        hp, sp = _layer(hp, *lw, None)
        hs, ss = _layer(hs, *lw, (state_a[l], state_b1[l], state_b2[l], state_b3[l], state_c[l]))
        new_p.append(sp)
        new_s.append(ss)
    y_prompt = _rmsnorm(hp, final_norm_w)
    y_sample = _rmsnorm(hs, final_norm_w)
    return (y_prompt, y_sample,
            _stack_layers(new_p, 0), _stack_layers(new_s, 0),
            _stack_layers(new_p, 1), _stack_layers(new_s, 1),
            _stack_layers(new_p, 2), _stack_layers(new_s, 2),
            _stack_layers(new_p, 3), _stack_layers(new_s, 3),
            _stack_layers(new_p, 4), _stack_layers(new_s, 4))
```

```python
import numpy as np
from contextlib import ExitStack
import concourse.bass as bass
import concourse.mybir as mybir
from concourse.bass_utils import run_bass_kernel_spmd

F32 = mybir.dt.float32
BF16 = mybir.dt.bfloat16
AF = mybir.ActivationFunctionType
ALU = mybir.AluOpType

D = 1024
T = 2048
DIN = 7424
NCORES = 8
EPS = 1e-6
C_QA, C_KA, C_VA, C_GA = 0, 512, 640, 768
C_QB, C_KB, C_VB, C_GB = 1280, 2048, 2816, 3584
C_UC, C_GC, C_MA, C_MB, C_MC = 3840, 4096, 4352, 5376, 6400
B_PAIRS = ((128, 1), (512, 4), (2048, 16))


def _slopes(n):
    return 2.0 ** (-8.0 * (np.arange(n, dtype=np.float64) + 1.0) / n)


def _tables():
    sa = _slopes(8)
    sb = _slopes(12)
    k = np.arange(128)[:, None].astype(np.float64)
    q = np.arange(128)[None, :].astype(np.float64)
    out = []
    for ch in range(10):
        if ch < 4:
            cs = [sa[2 * ch], sa[2 * ch + 1]]
            dil = 1
        else:
            g, jc = (ch - 4) // 2, (ch - 4) % 2
            dil = B_PAIRS[g][1]
            cs = [sb[g * 4 + 2 * jc] * dil, sb[g * 4 + 2 * jc + 1] * dil]
        tp = np.zeros((128, 2, 2, 128))
        ts = np.zeros((128, 4, 2))
        tn = np.zeros((4, 4, 2))
        for h in range(2):
            c = cs[h]
            dd = q - k
            tp[:, h, 0, :] = np.where(dd >= 0, np.exp(-c * np.maximum(dd, 0)), 0.0)
            do = 128 + q - k
            tp[:, h, 1, :] = np.where(do <= 128, np.exp(-c * np.minimum(do, 128.0)), 0.0)
            for j in range(4):
                m = np.arange(128)
                if dil == 1:
                    st = 128 + j - m
                    ts[:, j, h] = np.where(m >= j, np.exp(-c * st), 0.0)
                    for i in range(4):
                        tn[i, j, h] = np.exp(-c * (j - i)) if i <= j else 0.0
                else:
                    ts[:, j, h] = np.exp(-c * (128 - m))
                    tn[j, j, h] = 1.0
        out.append((tp.reshape(128, 512), ts.reshape(128, 8), tn))
    return out


class Prog:
    def __init__(self, nc, es):
        self.nc = nc
        self.es = es
        self.eng = {"pe": nc.tensor, "act": nc.scalar, "dve": nc.vector, "pool": nc.gpsimd, "sp": nc.sync}
        self.sem = {e: es.enter_context(nc.semaphore("c_" + e)) for e in self.eng}
        self.cnt = {e: 0 for e in self.eng}
        self.waited = {e: {} for e in self.eng}
        self.semobj = {}
        self.ndsem = 0

    def dsem(self, name):
        s = self.es.enter_context(self.nc.semaphore("d_" + name))
        self.ndsem += 1
        return [s, 0]

    def _wait(self, e, toks):
        best = {}
        for (s, v) in toks:
            k = id(s)
            self.semobj[k] = s
            if v > best.get(k, 0):
                best[k] = v
        for k, v in best.items():
            if self.waited[e].get(k, 0) < v:
                self.eng[e].wait_ge(self.semobj[k], v)
                self.waited[e][k] = v

    def _deps(self, e, reads, writes):
        toks = []
        mysem = self.sem[e]
        skip = (e == "pe")
        for r in reads:
            if r.w is not None:
                if r.w[0] is mysem and skip:
                    continue
                toks.append(r.w)
        for w in writes:
            if w.w is not None and not (w.w[0] is mysem and skip):
                toks.append(w.w)
            for t in w.r.values():
                if not (t[0] is mysem and skip):
                    toks.append(t)
        return toks

    def _mark(self, tok, reads, writes):
        for w in writes:
            w.w = tok
            w.r = {}
        for r in reads:
            k = id(tok[0])
            if k not in r.r or r.r[k][1] < tok[1]:
                r.r[k] = tok

    def op(self, e, fn, reads=(), writes=()):
        self._wait(e, self._deps(e, reads, writes))
        ins = fn(self.eng[e])
        self.cnt[e] += 1
        ins.then_inc(self.sem[e], 1)
        self._mark((self.sem[e], self.cnt[e]), reads, writes)

    def group(self, fns, reads=(), writes=()):
        e = "pe"
        self._wait(e, self._deps(e, reads, writes))
        ins = None
        for fn in fns:
            ins = fn(self.eng[e])
        self.cnt[e] += 1
        ins.then_inc(self.sem[e], 1)
        self._mark((self.sem[e], self.cnt[e]), reads, writes)

    def dma(self, q, ds, out, in_, reads=(), writes=()):
        toks = self._deps(q, reads, writes)
        if ds is None:
            ring = self.ring[q]
            ds = ring[self.ridx[q] % len(ring)]
            self.ridx[q] += 1
            if ds[1] > 0:
                toks.append((ds[0], ds[1]))
        self._wait(q, toks)
        ins = self.eng[q].dma_start(out=out, in_=in_)
        ds[1] += 16
        ins.then_inc(ds[0], 16)
        self._mark((ds[0], ds[1]), reads, writes)

    def mkrings(self, nsp, npool):
        self.ring = {"sp": [self.dsem("rs%d" % i) for i in range(nsp)], "pool": [self.dsem("rp%d" % i) for i in range(npool)]}
        self.ridx = {"sp": 0, "pool": 0}

    def all_dma_toks(self):
        return [(d[0], d[1]) for q in self.ring for d in self.ring[q] if d[1] > 0]

    def wait_all(self, e, toks):
        self._wait(e, toks)

    def barrier(self, dsems):
        toks = [(self.sem[f], self.cnt[f]) for f in self.eng if self.cnt[f] > 0]
        toks += [(d[0], d[1]) for d in dsems if d[1] > 0]
        toks += self.all_dma_toks()
        for e in self.eng:
            self._wait(e, toks)


def mkap(ref, dims):
    return bass.AP(ref.tensor, ref.offset, [list(ref.ap[0])] + [list(d) for d in dims])


class StopBuild(Exception):
    pass


STOP = None


def ckpt(name):
    if STOP == name:
        raise StopBuild()


class Res:
    __slots__ = ("w", "r", "n")

    def __init__(self, n=""):
        self.w = None
        self.r = {}
        self.n = n


def build(DEPTH, NS):
    TS = NS * 4
    TT = T + TS
    nc = bass.Bass("TRN2", target_bir_lowering=False)

    def din(name, shape):
        return nc.dram_tensor(name, list(shape), F32, kind="ExternalInput").ap()

    def dout(name, shape):
        return nc.dram_tensor(name, list(shape), F32, kind="ExternalOutput").ap()

    xp = din("xp", (T, D))
    xs = din("xs", (TS, D))
    sa = din("sa", (DEPTH, NS, 2, 128, 128))
    sb = [din("sb1", (DEPTH, NS, 2, 128, 256)), din("sb2", (DEPTH, NS, 2, 512, 256)),
          din("sb3", (DEPTH, NS, 2, 2048, 256))]
    sc = din("sc", (DEPTH, NS, 15, 256))
    nw = din("nw", (DEPTH, 128, 8))
    fnw = din("fnw", (128, D))
    w_in = din("w_in", (DEPTH, D, DIN))
    sinkx = din("sinkx", (DEPTH, 128, 4))
    wpa = din("wpa", (DEPTH, 512, D))
    wpb = din("wpb", (DEPTH, 256, D))
    wpc = din("wpc", (DEPTH, 256, D))
    wpool = din("wpool", (DEPTH, 4, 64, 64))
    pscale = din("pscale", (DEPTH, 128, 2))
    w_out = din("w_out", (DEPTH, D, D))
    tabs = din("tabs", (10, 128, 648))
    consts = din("consts", (128, 512))

    yp = dout("yp", (T, D))
    ys = dout("ys", (TS, D))
    nap = dout("nap", (DEPTH, 2, 128, 128))
    nas = dout("nas", (DEPTH, NS, 2, 128, 128))
    nbp = [dout("nb1p", (DEPTH, 2, 128, 256)), dout("nb2p", (DEPTH, 2, 512, 256)), dout("nb3p", (DEPTH, 2, 2048, 256))]
    nbs = [dout("nb1s", (DEPTH, NS, 2, 128, 256)), dout("nb2s", (DEPTH, NS, 2, 512, 256)),
           dout("nb3s", (DEPTH, NS, 2, 2048, 256))]
    ncp = dout("ncp", (DEPTH, 15, 256))
    ncs = dout("ncs", (DEPTH, NS, 15, 256))

    es = ExitStack()
    with es:
        def sb_t(name, shape, dt):
            return es.enter_context(nc.sbuf_tensor(name, list(shape), dt))

        P = Prog(nc, es)
        x = sb_t("x", (128, 17, D), F32)
        hT = sb_t("hT", (128, 8, TT), BF16)
        br = sb_t("br", (128, 8, TT), BF16)
        NSB = 2 if NS >= 2 else 1
        ATT_BYTES = 2 * ((TT * 2 + 3) // 4 * 4) + 17 * 2 * 128 * 2 + 2 * TT * 4 + 2048 + 3072 + 3 * 2048 + (NSB * 4 * 128 * 2) * 3 + 64 * 4 * 2
        SCR = max(ATT_BYTES, 8 * TT * 2, 3 * 2400 * 4 + TT * 2)
        scr = sb_t("scr", (128, SCR // 4), F32)
        hb = sb_t("hb", (128, D), BF16)
        wring = [sb_t("wr%d" % i, (128, 8, 256), BF16) for i in range(3)]
        stage = sb_t("stage", (128, 2, 256), F32)
        tab = sb_t("tab", (128, 648), F32)
        cst = sb_t("cst", (128, 160), F32)
        cbf = sb_t("cbf", (128, 384), BF16)
        small = sb_t("small", (128, 64), F32)
        nwt = sb_t("nwt", (128, 8), F32)
        sinkt = sb_t("sinkt", (128, 4), F32)
        pst = sb_t("pst", (128, 2), F32)
        wpl = sb_t("wpl", (128, 2, 128), BF16)
        qblk = sb_t("qblk", (128, 64, 2), BF16)
        VS = sb_t("VS", (128, 128), BF16)
        onesAllT = sb_t("onesAllT", (128, 128), BF16)
        psall = es.enter_context(nc.psum_tensor("psall", [128, 4096], F32))
        banks = [psall[:, i * 512:(i + 1) * 512] for i in range(8)]

        off = [0]

        def view(nbytes, dt, shape):
            esz = 4 if dt == F32 else 2
            o = off[0]
            off[0] += (nbytes + 3) // 4 * 4
            v = scr[:, o // 4:(o + nbytes + 3) // 4]
            if dt != F32:
                v = v.bitcast(dt)
            v = v[:, 0:nbytes // esz]
            if len(shape) == 3:
                v = v.rearrange("p (a b) -> p a b", a=shape[1])
            elif len(shape) == 4:
                v = v.rearrange("p (a b c) -> p a b c", a=shape[1], b=shape[2])
            return v
        QT = view(TT * 2, BF16, (128, TT))
        KT = view(TT * 2, BF16, (128, TT))
        Vpad = view(17 * 2 * 128 * 2, BF16, (128, 17, 2, 128))
        UZ = view(2 * TT * 4, F32, (128, 2, TT))
        Ebuf = view(2048, F32, (128, 512))
        PT = view(3072, BF16, (128, 3, 512))
        off_tA = off[0]
        tA = view(2048, F32, (128, 512))
        tB = view(2048, F32, (128, 512))
        tC = view(2048, F32, (128, 512))
        NTL = NSB * 4
        Kt = view(NTL * 128 * 2, BF16, (128, NTL, 128))
        Vt = view(NTL * 128 * 2, BF16, (128, NTL, 128))
        KTs = view(NTL * 128 * 2, BF16, (128, NTL, 128))
        Pbuf = view(64 * 4 * 2, BF16, (128, 2, 128))
        assert off[0] <= SCR, (off[0], SCR)
        mT = scr[:, 0:8 * TT // 2].bitcast(BF16).rearrange("p (a b) -> p a b", a=8)
        LB = 2400
        Ub = scr[:, 0:LB]
        Sa = scr[:, LB:2 * LB]
        Sb_ = scr[:, 2 * LB:3 * LB]
        diffT = scr[:, 3 * LB:3 * LB + TT // 2].bitcast(BF16)
        yb = scr[:, off_tA // 4:off_tA // 4 + 1024]
        fnwb = scr[:, 0:1024]

        R = {}

        def res(n):
            if n not in R:
                R[n] = Res(n)
            return R[n]
        Rb = [res("bank%d" % i) for i in range(8)]
        Rx = [res("x%d" % i) for i in range(17)]
        Rw = [res("w%d" % i) for i in range(3)]
        P.mkrings(24, 16)
        d_cp = P.dsem("cp")
        wctr = [0]

        ident = cbf[:, 0:128]
        onesE = cbf[:, 128:256]
        onesO = cbf[:, 256:384]

        def load_w(l, src, r0, nrows, c0, ncols, dcol=0, slot=None, kch=None):
            if slot is None:
                slot = wctr[0] % 3
                wctr[0] += 1
            kch = nrows // 128
            s = src[l, r0:r0 + nrows, c0:c0 + ncols].rearrange("(k p) c -> p k c", p=128)
            P.dma("pool", None, wring[slot][:, 0:kch, dcol:dcol + ncols], s, writes=[Rw[slot]])
            return slot

        def evac(e, out, in_, reads, writes, scale=None, func=None):
            if e == "act":
                kw = {}
                if scale is not None:
                    kw["scale"] = scale
                P.op("act", lambda a: a.activation(out=out, in_=in_, func=func or AF.Copy, **kw), reads, writes)
            else:
                if scale is not None:
                    P.op("dve", lambda v: v.tensor_scalar(out=out, in0=in_, scalar1=scale, scalar2=None, op0=ALU.mult), reads, writes)
                else:
                    P.op("dve", lambda v: v.tensor_copy(out=out, in_=in_), reads, writes)

        TCS = [(0, 512), (512, 512), (1024, 512), (1536, 512), (T, TS)]
        pbank = [0]

        def next_pbank():
            b = 6 + (pbank[0] % 2)
            pbank[0] += 1
            return b

        def fm_proj(slot, c0, kch, rhs_fn, t0, ln, b, extra_reads=()):
            fns = []
            for k in range(kch):
                fns.append(lambda t, k=k: t.matmul(banks[b][:, 0:ln], lhsT=wring[slot][:, k, c0:c0 + 128],
                                                   rhs=rhs_fn(k, t0, ln), start=(k == 0), stop=(k == kch - 1)))
            P.group(fns, reads=[Rw[slot]] + list(extra_reads), writes=[Rb[b]])

        def hT_rhs(k, t0, ln):
            return hT[:, k, t0:t0 + ln]

        Rh = res("hT")

        P.dma("sp", None, cst[:, 0:128], consts[:, 0:128], writes=[res("cst")])
        P.dma("sp", None, cst[:, 128:160], consts[:, 384:416], writes=[res("cst")])
        P.dma("pool", None, cbf[:], consts[:, 0:384], writes=[res("cbf")])
        for tt in range(16):
            P.dma("sp", None, x[:, tt, :], xp[tt * 128:(tt + 1) * 128, :], writes=[Rx[tt]])
        P.dma("sp", None, x[0:TS, 16, :], xs[:, :], writes=[Rx[16]])
        P.op("pool", lambda g: g.memset(qblk[:], 0.0), writes=[res("qblk")])
        def state_copies(l):
            for (src, dst, W) in [(sa, nas, 128), (sb[0], nbs[0], 128), (sb[1], nbs[1], 512)]:
                s = src[l, :, :, 4:W, :].rearrange("n a r c -> (n a) (r c)")
                d = dst[l, :, :, 0:W - 4, :].rearrange("n a r c -> (n a) (r c)")
                P.dma("act", d_cp, d, s)
            for n in range(NS):
                s = sb[2][l, n, :, 4:2048, :].rearrange("a r c -> a (r c)")
                d = nbs[2][l, n, :, 0:2044, :].rearrange("a r c -> a (r c)")
                P.dma("act", d_cp, d, s)
            P.dma("act", d_cp, ncs[l, :, 0:11, :].rearrange("n r c -> n (r c)"), sc[l, :, 4:15, :].rearrange("n r c -> n (r c)"))
        ssq = small[:, 0:17]
        rstd = small[:, 17:34]
        mhalf = small[:, 34:51]
        P.op("dve", lambda v: v.memset(mhalf, -0.5), writes=[res("mhalf")])
        P.op("dve", lambda v: v.memset(Vpad[:, :, :, :], 0.0), writes=[res("Vpad")])
        P.op("dve", lambda v: v.memset(ssq, 1.0), writes=[res("ssq")])

        def rmsnorm_stats():
            for tt in range(17):
                rows = 128 if tt < 16 else TS
                P.op("act", lambda a: a.activation(out=hb[0:rows, :], in_=x[0:rows, tt, :], func=AF.Square, accum_out=ssq[0:rows, tt:tt + 1]),
                     reads=[Rx[tt]], writes=[res("hb"), res("ssq")])
            P.op("dve", lambda v: v.tensor_scalar(out=rstd, in0=ssq, scalar1=1.0 / D, scalar2=EPS, op0=ALU.mult, op1=ALU.add),
                 reads=[res("ssq")], writes=[res("rstd")])
            P.op("pool", lambda g: g.tensor_tensor(out=rstd, in0=rstd, in1=mhalf, op=ALU.pow),
                 reads=[res("rstd"), res("mhalf")], writes=[res("rstd")])
        ALLD = []
        P.op("pool", lambda g: g.memset(onesAllT[:], 1.0), writes=[res("onesAll")])
        stg = [0]

        def tm_mm(slot, c0, ncols, tok, rows):
            b = next_pbank()
            fns = []
            for k in range(8):
                fns.append(lambda t, k=k: t.matmul(banks[b][0:rows, 0:ncols], lhsT=tok(k),
                                                   rhs=wring[slot][:, k, c0:c0 + ncols], start=(k == 0), stop=(k == 7)))
            P.group(fns, reads=[Rw[slot], Rh], writes=[Rb[b]])
            return b

        def stage_out(b, rows, ncols, dsts, r0=0):
            s = stg[0] % 2
            stg[0] += 1
            rs = res("stage%d" % s)
            evac("dve", stage[0:rows, s, 0:ncols], banks[b][0:rows, 0:ncols], [Rb[b]], [rs])
            for dd in dsts:
                a0, a1, dst = dd[0:3]
                c0, c1 = (dd[3], dd[4]) if len(dd) > 3 else (0, ncols)
                P.dma("sp", None, dst, stage[a0:a1, s, c0:c1], reads=[rs])

        def gate(l, sg, c0, t0, ln):
            b = next_pbank()
            fm_proj(sg, c0, 8, hT_rhs, t0, ln, b, [Rh])
            P.op("act", lambda a: a.activation(out=tB[:, 0:ln], in_=banks[b][:, 0:ln], func=AF.Tanh, scale=0.5), [Rb[b]], [res("tB")])
            P.op("dve", lambda v: v.scalar_tensor_tensor(out=tA[:, 0:ln], in0=tB[:, 0:ln], scalar=1.0, in1=banks[b][:, 0:ln],
                                                         op0=ALU.add, op1=ALU.mult), [res("tB"), Rb[b]], [res("tA")])

        def uz_dst(which, it, d):
            c0, r, i0 = it
            if d == 1:
                return UZ[:, which, i0:i0 + 128]
            base = UZ[:, which, r + d * i0:r + d * i0 + 1]
            return mkap(base, [[d, 128]])

        def finalize(l, cur, d, mode, sg, br_k):
            bu, bz, lst = cur
            if mode in ("B0", "B1"):
                for it in lst:
                    c0 = it[0]
                    for which, bk in ((0, bu), (1, bz)):
                        dst = uz_dst(which, it, d)
                        if mode == "B0":
                            evac("act" if which == 0 else "dve", dst, banks[bk][:, c0:c0 + 128], [Rb[bk]], [res("UZ")])
                        else:
                            P.op("dve", lambda v, dst=dst, bk=bk, c0=c0: v.tensor_tensor(out=dst, in0=dst, in1=banks[bk][:, c0:c0 + 128], op=ALU.add),
                                 [Rb[bk], res("UZ")], [res("UZ")])
                return
            t0 = lst[0][2]
            gate(l, sg, 0, t0, 512)
            if mode == "A":
                P.op("dve", lambda v: v.tensor_scalar(out=tB[:, :], in0=banks[bz][:, :], scalar1=sinkt[:, br_k:br_k + 1], scalar2=None, op0=ALU.add),
                     [Rb[bz], res("sinkt"), res("tB")], [res("tB")])
                P.op("dve", lambda v: v.reciprocal(out=tB[:, :], in_=tB[:, :]), [res("tB")], [res("tB")])
                P.op("dve", lambda v: v.scalar_tensor_tensor(out=tC[:, :], in0=banks[bu][:, :], scalar=0.5, in1=tB[:, :], op0=ALU.mult, op1=ALU.mult),
                     [Rb[bu], res("tB")], [res("tC")])
            else:
                P.op("dve", lambda v: v.tensor_tensor(out=tB[:, :], in0=UZ[:, 1, t0:t0 + 512], in1=banks[bz][:, :], op=ALU.add),
                     [Rb[bz], res("UZ"), res("tB")], [res("tB")])
                P.op("dve", lambda v: v.reciprocal(out=tB[:, :], in_=tB[:, :]), [res("tB")], [res("tB")])
                P.op("dve", lambda v: v.tensor_tensor(out=tC[:, :], in0=UZ[:, 0, t0:t0 + 512], in1=banks[bu][:, :], op=ALU.add),
                     [Rb[bu], res("UZ")], [res("tC")])
                P.op("dve", lambda v: v.scalar_tensor_tensor(out=tC[:, :], in0=tC[:, :], scalar=0.5, in1=tB[:, :], op0=ALU.mult, op1=ALU.mult),
                     [res("tC"), res("tB")], [res("tC")])
            P.op("dve", lambda v: v.tensor_tensor(out=br[:, br_k, t0:t0 + 512], in0=tC[:, :], in1=tA[:, :], op=ALU.mult),
                 [res("tC"), res("tA")], [res("br%d" % br_k)])

        def sample_attention(l, d, state, st_c0, st_dup, mode, sg, br_k):
            qs = QT[:, T:T + TS]
            P.op("dve", lambda v: v.tensor_copy(out=qblk[0:64, 0:TS, 0], in_=qs[0:64, :]), [res("QT")], [res("qblk")])
            P.op("dve", lambda v: v.tensor_copy(out=qblk[64:128, 0:TS, 1], in_=qs[64:128, :]), [res("QT")], [res("qblk")])
            ncol = TS * 2
            bS, bN, bU, bZ = 0, 1, 2, 3
            qb2 = qblk[:, 0:TS, :].rearrange("p u h -> p (u h)")
            P.group([lambda t: t.matmul(banks[bN][0:TS, 0:ncol], lhsT=KT[:, T:T + TS], rhs=qb2, start=True, stop=True)],
                    reads=[res("KT"), res("qblk")], writes=[Rb[bN]])
            pn = Pbuf[0:TS, 1, 0:ncol]
            P.op("act", lambda a: a.activation(out=tB[0:TS, 0:ncol], in_=banks[bN][0:TS, 0:ncol], func=AF.Exp), [Rb[bN]], [res("tB")])
            P.op("dve", lambda v: v.tensor_tensor(out=pn, in0=tB[0:TS, 0:ncol], in1=tab[0:TS, 520:520 + ncol], op=ALU.mult),
                 [res("tB"), res("tab")], [res("Pn")])
            npn = 1 if d == 1 else 4
            for n0 in range(0, NS, NSB):
                nn = min(NSB, NS - n0)
                for kv, dstt, rn in ((0, Kt, "Kt"), (1, Vt, "Vt")):
                    for n in range(n0, n0 + nn):
                        for jt in range(npn):
                            ti = (n - n0) * npn + jt
                            wcols = 64 if st_dup else 128
                            for half in range(2 if st_dup else 1):
                                src = state[l, n, kv, jt:jt + d * 127 + 1:d, st_c0:st_c0 + wcols]
                                P.dma("pool", None, dstt[:, ti, half * 64:half * 64 + wcols], src, writes=[res(rn)])
                ntl = nn * npn
                b = next_pbank()
                pb16 = banks[b][:, :].bitcast(BF16)
                fns = []
                for ti in range(ntl):
                    fns.append(lambda t, ti=ti: t.transpose(out=pb16[:, ti * 128:(ti + 1) * 128], in_=Kt[:, ti, :], identity=ident))
                P.group(fns, reads=[res("Kt"), res("cbf")], writes=[Rb[b]])
                evac("dve", KTs[:, 0:ntl, :].rearrange("p a b -> p (a b)"), pb16[:, 0:ntl * 128], [Rb[b]], [res("KTs")])
                cols = []
                for n in range(n0, n0 + nn):
                    for jt in range(npn):
                        ti = (n - n0) * npn + jt
                        cols.append((ti, n * 8, 8) if d == 1 else (ti, (n * 4 + jt) * 2, 2))
                fns = [lambda t, ti=ti, cc0=cc0, cn=cn: t.matmul(banks[bS][:, cc0:cc0 + cn], lhsT=KTs[:, ti, :], rhs=qb2[:, cc0:cc0 + cn],
                                                                 start=True, stop=True, skip_group_check=True) for (ti, cc0, cn) in cols]
                P.group(fns, reads=[res("KTs"), res("qblk")], writes=[Rb[bS]])
                c_lo, c_hi = n0 * 8, (n0 + nn) * 8
                P.op("act", lambda a, c_lo=c_lo, c_hi=c_hi: a.activation(out=tC[:, c_lo:c_hi], in_=banks[bS][:, c_lo:c_hi], func=AF.Exp), [Rb[bS]], [res("tC")])
                tsb = tab[:, 512:520]
                tsb3 = mkap(tsb, [[0, nn], [1, 8]])
                P.op("dve", lambda v, c_lo=c_lo, c_hi=c_hi, tsb3=tsb3, nn=nn: v.tensor_tensor(
                    out=Pbuf[:, 0, c_lo:c_hi].rearrange("p (n c) -> p n c", n=nn), in0=tC[:, c_lo:c_hi].rearrange("p (n c) -> p n c", n=nn),
                    in1=tsb3, op=ALU.mult), [res("tC"), res("tab")], [res("Pb")])
                fu, fz = [], []
                if n0 == 0:
                    fu.append(lambda t: t.matmul(banks[bU][:, 0:ncol], lhsT=VS[0:TS, :], rhs=pn, start=True, stop=False, skip_group_check=True))
                    fz.append(lambda t: t.matmul(banks[bZ][:, 0:ncol], lhsT=onesAllT[0:TS, :], rhs=pn, start=True, stop=False, skip_group_check=True))
                for (ti, cc0, cn) in cols:
                    fu.append(lambda t, ti=ti, cc0=cc0, cn=cn: t.matmul(banks[bU][:, cc0:cc0 + cn], lhsT=Vt[:, ti, :], rhs=Pbuf[:, 0, cc0:cc0 + cn],
                                                                          start=False, stop=True, skip_group_check=True))
                fz.append(lambda t, c_lo=c_lo, c_hi=c_hi: t.matmul(banks[bZ][:, c_lo:c_hi], lhsT=onesAllT[:, :], rhs=Pbuf[:, 0, c_lo:c_hi],
                                                                     start=False, stop=True, skip_group_check=True))
                P.group(fu, reads=[res("Vt"), res("Pb"), res("Pn"), res("VS")], writes=[Rb[bU]])
                P.group(fz, reads=[res("Pb"), res("Pn"), res("onesAll")], writes=[Rb[bZ]])
            u3 = banks[bU][:, 0:ncol].rearrange("p (u h) -> p u h", h=2)
            z3 = banks[bZ][:, 0:ncol].rearrange("p (u h) -> p u h", h=2)
            halves = ((0, 64, 0), (64, 128, 1))
            if mode in ("B0", "B1"):
                for (p0, p1, h) in halves:
                    for which, src in ((0, u3), (1, z3)):
                        dst = UZ[p0:p1, which, T:T + TS]
                        bk = bU if which == 0 else bZ
                        if mode == "B0":
                            evac("dve", dst, src[p0:p1, :, h], [Rb[bk]], [res("UZ")])
                        else:
                            P.op("dve", lambda v, dst=dst, src=src, p0=p0, p1=p1, h=h: v.tensor_tensor(out=dst, in0=dst, in1=src[p0:p1, :, h], op=ALU.add),
                                 [Rb[bk], res("UZ")], [res("UZ")])
                return
            gate(l, sg, 0, T, TS)
            for (p0, p1, h) in halves:
                if mode == "A":
                    P.op("dve", lambda v, p0=p0, p1=p1, h=h: v.tensor_scalar(out=tB[p0:p1, 0:TS], in0=z3[p0:p1, :, h], scalar1=sinkt[p0:p1, br_k:br_k + 1],
                                                                            scalar2=None, op0=ALU.add), [Rb[bZ], res("sinkt"), res("tB")], [res("tB")])
                    P.op("dve", lambda v, p0=p0, p1=p1: v.reciprocal(out=tB[p0:p1, 0:TS], in_=tB[p0:p1, 0:TS]), [res("tB")], [res("tB")])
                    P.op("dve", lambda v, p0=p0, p1=p1, h=h: v.scalar_tensor_tensor(out=tC[p0:p1, 0:TS], in0=u3[p0:p1, :, h], scalar=0.5, in1=tB[p0:p1, 0:TS],
                                                                                   op0=ALU.mult, op1=ALU.mult), [Rb[bU], res("tB"), res("tC")], [res("tC")])
                else:
                    P.op("dve", lambda v, p0=p0, p1=p1, h=h: v.tensor_tensor(out=tB[p0:p1, 0:TS], in0=UZ[p0:p1, 1, T:T + TS], in1=z3[p0:p1, :, h], op=ALU.add),
                         [Rb[bZ], res("UZ"), res("tB")], [res("tB")])
                    P.op("dve", lambda v, p0=p0, p1=p1: v.reciprocal(out=tB[p0:p1, 0:TS], in_=tB[p0:p1, 0:TS]), [res("tB")], [res("tB")])
                    P.op("dve", lambda v, p0=p0, p1=p1, h=h: v.tensor_tensor(out=tC[p0:p1, 0:TS], in0=UZ[p0:p1, 0, T:T + TS], in1=u3[p0:p1, :, h], op=ALU.add),
                         [Rb[bU], res("UZ"), res("tC")], [res("tC")])
                    P.op("dve", lambda v, p0=p0, p1=p1: v.scalar_tensor_tensor(out=tC[p0:p1, 0:TS], in0=tC[p0:p1, 0:TS], scalar=0.5, in1=tB[p0:p1, 0:TS],
                                                                             op0=ALU.mult, op1=ALU.mult), [res("tC"), res("tB")], [res("tC")])
            P.op("dve", lambda v: v.tensor_tensor(out=br[:, br_k, T:T + TS], in0=tC[:, 0:TS], in1=tA[:, 0:TS], op=ALU.mult),
                 [res("tC"), res("tA")], [res("br%d" % br_k)])

        def attention(l, ch, seqs, d, q_c0, k_c0, v_c0, dup, g_c0, state, st_c0, W, kout, kouts, out_c0, do_out, mode, br_k):
            P.dma("sp", None, tab[:], tabs[ch, :, :], writes=[res("tab")])
            L = T // d
            sq = load_w(l, w_in, 0, D, q_c0, 128)
            sk = wctr[0] % 3
            wctr[0] += 1
            if dup:
                for dc_, c_ in ((0, k_c0), (64, k_c0), (128, v_c0), (192, v_c0)):
                    load_w(l, w_in, 0, D, c_, 64, dc_, slot=sk)
            else:
                load_w(l, w_in, 0, D, k_c0, 128, 0, slot=sk)
                load_w(l, w_in, 0, D, v_c0, 128, 128, slot=sk)
            for (dst, rn, slot, scl, e) in [(QT, "QT", sq, 0.125, "act"), (KT, "KT", sk, None, "dve")]:
                for (t0, ln) in TCS:
                    b = next_pbank()
                    fm_proj(slot, 0, 8, hT_rhs, t0, ln, b, [Rh])
                    if t0 < T and d > 1:
                        src = banks[b][:, 0:ln].rearrange("p (a r) -> p r a", r=d)
                        o1 = dst[:, t0 // d:t0 // d + 1]
                        dd = mkap(o1, [[L, d], [1, ln // d]])
                    else:
                        src = banks[b][:, 0:ln]
                        dd = dst[:, t0:t0 + ln]
                    evac(e, dd, src, [Rb[b]], [res(rn)], scale=scl)
            ckpt('att1')
            ckpt('att1' + mode)
            tiles = []
            for (r, Ls) in seqs:
                for bb in range(Ls // 128):
                    tiles.append((r, bb))
            nv = 64 if dup else 128
            for ti in range(17):
                if ti < 16:
                    r, bb = tiles[ti]
                    start = r + d * 128 * bb
                    rows = 128
                    tok = (lambda k, start=start: hT[:, k, start:start + d * 127 + 1:d])
                else:
                    rows = TS
                    tok = (lambda k: hT[:, k, T:T + TS])
                b = tm_mm(sk, 128, 128, tok, rows)
                evac("dve", Vpad[0:rows, ti, 0, 0:64], banks[b][0:rows, 0:64], [Rb[b]], [res("Vpad")])
                evac("act", Vpad[0:rows, ti, 1, 64:128], banks[b][0:rows, (0 if dup else 64):(64 if dup else 128)], [Rb[b]], [res("Vpad")])
                if ti == 16:
                    evac("act", VS[0:TS, :], banks[b][0:TS, 0:128], [Rb[b]], [res("VS")])
                    if do_out:
                        stage_out(b, TS, nv, [(4 * n, 4 * n + 4, kouts[l, n, 1, W - 4:W, out_c0:out_c0 + nv]) for n in range(NS)])
                elif do_out and d == 1 and start >= T - W:
                    s0 = start - (T - W)
                    stage_out(b, 128, nv, [(0, 128, kout[l, 1, s0:s0 + 128, out_c0:out_c0 + nv])])
            ckpt('att2')
            ckpt('att2' + mode)
            if do_out:
                for tt in range(16):
                    if tt * 128 >= T - W:
                        s0 = tt * 128 - (T - W)
                        if d > 1:
                            b = tm_mm(sk, 0, 256, (lambda k, tt=tt: hT[:, k, tt * 128:(tt + 1) * 128]), 128)
                            stage_out(b, 128, 256, [(0, 128, kout[l, 0, s0:s0 + 128, out_c0:out_c0 + 128], 0, 128),
                                                    (0, 128, kout[l, 1, s0:s0 + 128, out_c0:out_c0 + 128], 128, 256)])
                        else:
                            b = tm_mm(sk, 0, nv, (lambda k, tt=tt: hT[:, k, tt * 128:(tt + 1) * 128]), 128)
                            stage_out(b, 128, nv, [(0, 128, kout[l, 0, s0:s0 + 128, out_c0:out_c0 + nv])])
                b = tm_mm(sk, 0, nv, (lambda k: hT[:, k, T:T + TS]), TS)
                stage_out(b, TS, nv, [(4 * n, 4 * n + 4, kouts[l, n, 0, W - 4:W, out_c0:out_c0 + nv]) for n in range(NS)])
            ckpt('att3')
            ckpt('att3' + mode)
            sg = load_w(l, w_in, 0, D, g_c0, 128)
            RQ, RK, RV = res("QT"), res("KT"), res("Vpad")
            uzb = 0
            win_cols = 0
            cur = None
            kbg = 0
            for (r, Ls) in seqs:
                nb = Ls // 128
                base = (r * L) if d > 1 else 0
                for kb in range(nb):
                    k0 = base + kb * 128
                    nq = 256 if kb < nb - 1 else 128
                    fns = [lambda t, h=h, k0=k0, nq=nq: t.matmul(
                        banks[h][:, 0:nq], lhsT=KT[h * 64:(h + 1) * 64, k0:k0 + 128],
                        rhs=QT[h * 64:(h + 1) * 64, k0:k0 + nq], start=True, stop=True, skip_group_check=True) for h in range(2)]
                    P.group(fns, reads=[RQ, RK], writes=[Rb[0], Rb[1]])
                    s3 = psall[:, 0:1024].rearrange("p (h q) -> p h q", h=2)[:, :, 0:nq]
                    e3 = Ebuf.rearrange("p (h q) -> p h q", h=2)[:, :, 0:nq]
                    t3 = tab[:, 0:512].rearrange("p (h q) -> p h q", h=2)[:, :, 0:nq]
                    pslot = kbg % 3
                    p3 = PT[:, pslot, :].rearrange("p (h q) -> p h q", h=2)[:, :, 0:nq]
                    P.op("act", lambda a, s3=s3, e3=e3: a.activation(out=e3, in_=s3, func=AF.Exp), [Rb[0], Rb[1]], [res("E")])
                    P.op("dve", lambda v, e3=e3, t3=t3, p3=p3: v.tensor_tensor(out=p3, in0=e3, in1=t3, op=ALU.mult),
                         [res("E"), res("tab")], [res("PT%d" % pslot)])
                    if STOP == 'lp1':
                        kbg += 1
                        continue
                    if win_cols == 0:
                        cur = (2 + 2 * (uzb % 2), 3 + 2 * (uzb % 2), [])
                        uzb += 1
                    bu, bz, lst = cur
                    c0 = win_cols
                    parts = ([(kbg - 1, 128)] if kb > 0 else []) + [(kbg, 0)]
                    fu, fz = [], []
                    npart = 2 * len(parts)
                    idx = 0
                    for h in range(2):
                        for (tk, qo) in parts:
                            st_, sp_ = (idx == 0 and c0 == 0), (idx == npart - 1)
                            fu.append(lambda t, h=h, tk=tk, qo=qo, st_=st_, sp_=sp_, c0=c0, bu=bu: t.matmul(
                                banks[bu][:, c0:c0 + 128], lhsT=Vpad[:, tk, h, :], rhs=PT[:, tk % 3, h * 256 + qo:h * 256 + qo + 128],
                                start=st_, stop=sp_, skip_group_check=True))
                            fz.append(lambda t, h=h, tk=tk, qo=qo, st_=st_, sp_=sp_, c0=c0, bz=bz: t.matmul(
                                banks[bz][:, c0:c0 + 128], lhsT=(onesE if h == 0 else onesO), rhs=PT[:, tk % 3, h * 256 + qo:h * 256 + qo + 128],
                                start=st_, stop=sp_, skip_group_check=True))
                            idx += 1
                    rp = [res("PT%d" % (tk % 3)) for (tk, _) in parts]
                    P.group(fu, reads=rp + [RV], writes=[Rb[bu]])
                    P.group(fz, reads=rp + [res("cbf")], writes=[Rb[bz]])
                    lst.append((c0, r, kb * 128))
                    win_cols += 128
                    kbg += 1
                    if win_cols == 512:
                        if STOP != 'lp2':
                            finalize(l, cur, d, mode, sg, br_k)
                        win_cols = 0
            ckpt('att4')
            ckpt('att4' + mode)
            ckpt('lp1')
            ckpt('lp2')
            sample_attention(l, d, state, st_c0, dup, mode, sg, br_k)
            ckpt('end' + mode)

        def pool_phase(l):
            P.op("pool", lambda g: g.memset(wpl[:], 0.0), writes=[res("wpl")])
            for gi in range(4):
                p0 = 64 * (gi % 2)
                P.dma("pool", None, wpl[p0:p0 + 64, gi // 2, p0:p0 + 64], wpool[l, gi, :, :], writes=[res("wpl")])
            P.op("dve", lambda v: v.tensor_scalar(out=pst[:], in0=pst[:], scalar1=0.5, scalar2=None, op0=ALU.mult), [res("pst")], [res("pst")])
            s2 = load_w(l, w_in, 0, D, C_UC, 256)
            b = tm_mm(s2, 0, 256, (lambda k: hT[:, k, T - 128:T]), 128)
            stage_out(b, 128, 256, [(113, 128, ncp[l, :, :])])
            b = tm_mm(s2, 0, 256, (lambda k: hT[:, k, T:T + TS]), TS)
            stage_out(b, TS, 256, [(4 * n, 4 * n + 4, ncs[l, n, 11:15, :]) for n in range(NS)])
            SB0 = 16 + T
            NL = SB0 + 19 * NS
            for cc in range(2):
                su = load_w(l, w_in, 0, D, C_UC + 128 * cc, 128)
                sgc = load_w(l, w_in, 0, D, C_GC + 128 * cc, 128)
                RU = res("Ub")
                P.op("dve", lambda v: v.memset(Ub[:, 0:16], 0.0), writes=[RU])
                for (t0, ln) in TCS:
                    b = next_pbank()
                    fm_proj(su, 0, 8, hT_rhs, t0, ln, b, [Rh])
                    if t0 < T:
                        evac("dve", Ub[:, 16 + t0:16 + t0 + ln], banks[b][:, 0:ln], [Rb[b]], [RU])
                    else:
                        o1 = Ub[:, SB0 + 15:SB0 + 16]
                        dd = mkap(o1, [[19, NS], [1, 4]])
                        evac("dve", dd, banks[b][:, 0:TS].rearrange("p (n j) -> p n j", j=4), [Rb[b]], [RU])
                for n0 in range(0, NS, 8):
                    nn = min(8, NS - n0)
                    rows = nn * 15
                    P.dma("sp", None, tC[0:rows, 0:256], sc[l, n0:n0 + nn, :, :].rearrange("n r c -> (n r) c"), writes=[res("tC")])
                    b = next_pbank()
                    P.group([lambda t: t.transpose(out=banks[b][:, 0:rows], in_=tC[0:rows, cc * 128:(cc + 1) * 128], identity=cst[0:rows, 0:rows])],
                            reads=[res("tC"), res("cst")], writes=[Rb[b]])
                    o1 = Ub[:, SB0 + 19 * n0:SB0 + 19 * n0 + 1]
                    dd = mkap(o1, [[19, nn], [1, 15]])
                    evac("dve", dd, banks[b][:, 0:rows].rearrange("p (n r) -> p n r", r=15), [Rb[b]], [RU])
                RSa, RSb = res("Sa"), res("Sb")
                P.op("dve", lambda v: v.tensor_tensor(out=Sa[:, 1:NL], in0=Ub[:, 1:NL], in1=Ub[:, 0:NL - 1], op=ALU.add), [RU], [RSa])
                P.op("dve", lambda v: v.tensor_tensor(out=Sb_[:, 3:NL], in0=Sa[:, 3:NL], in1=Sa[:, 1:NL - 2], op=ALU.add), [RSa], [RSb])
                if cc == 1:
                    P.op("dve", lambda v: v.tensor_tensor(out=Sa[:, 7:NL], in0=Sb_[:, 7:NL], in1=Sb_[:, 3:NL - 4], op=ALU.add), [RSb, RSa], [RSa])
                    P.op("dve", lambda v: v.tensor_tensor(out=Sb_[:, 15:NL], in0=Sa[:, 15:NL], in1=Sa[:, 7:NL - 8], op=ALU.add), [RSa, RSb], [RSb])
                ws = (2, 4) if cc == 0 else (8, 16)
                RD = res("diffT")
                for hi, (S_, RS) in enumerate(((Sa, RSa), (Sb_, RSb))):
                    p0, p1 = 64 * hi, 64 * hi + 64
                    iw = 1.0 / ws[hi]
                    P.op("dve", lambda v, S_=S_, p0=p0, p1=p1, iw=iw: v.scalar_tensor_tensor(out=diffT[p0:p1, 0:T], in0=S_[p0:p1, 16:16 + T], scalar=iw,
                                                                                           in1=Ub[p0:p1, 16:16 + T], op0=ALU.mult, op1=ALU.subtract), [RS, RU], [RD])
                    o1 = S_[p0:p1, SB0 + 15:SB0 + 16]
                    sv = mkap(o1, [[19, NS], [1, 4]])
                    o2 = Ub[p0:p1, SB0 + 15:SB0 + 16]
                    uv = mkap(o2, [[19, NS], [1, 4]])
                    P.op("dve", lambda v, sv=sv, uv=uv, p0=p0, p1=p1, iw=iw: v.scalar_tensor_tensor(
                        out=diffT[p0:p1, T:T + TS].rearrange("p (n j) -> p n j", j=4), in0=sv, scalar=iw, in1=uv, op0=ALU.mult, op1=ALU.subtract), [RS, RU], [RD])
                    ic = cst[p0:p1, 128 + 16 * cc:144 + 16 * cc]
                    P.op("dve", lambda v, S_=S_, p0=p0, p1=p1, ic=ic: v.tensor_tensor(out=tC[p0:p1, 0:16], in0=S_[p0:p1, 16:32], in1=ic, op=ALU.mult),
                         [RS, res("cst"), res("tC")], [res("tC")])
                    P.op("dve", lambda v, p0=p0, p1=p1: v.tensor_tensor(out=diffT[p0:p1, 0:16], in0=tC[p0:p1, 0:16], in1=Ub[p0:p1, 16:32], op=ALU.subtract),
                         [res("tC"), RU, RD], [RD])
                for (t0, ln) in TCS:
                    b = next_pbank()
                    P.group([lambda t: t.matmul(banks[b][:, 0:ln], lhsT=wpl[:, cc, :], rhs=diffT[:, t0:t0 + ln], start=True, stop=True)],
                            reads=[res("wpl"), RD], writes=[Rb[b]])
                    gate(l, sgc, 0, t0, ln)
                    P.op("dve", lambda v: v.scalar_tensor_tensor(out=br[:, 6 + cc, t0:t0 + ln], in0=banks[b][:, 0:ln], scalar=pst[:, cc:cc + 1], in1=tA[:, 0:ln],
                                                                 op0=ALU.mult, op1=ALU.mult), [Rb[b], res("pst"), res("tA")], [res("br%d" % (6 + cc))])

        def final_phase(l):
            Rm = res("mT")
            for dc in range(8):
                sX = wctr[0] % 3
                sY = (wctr[0] + 1) % 3
                sZ = (wctr[0] + 2) % 3
                wctr[0] += 3
                load_w(l, w_in, 0, D, C_MA + 128 * dc, 128, 0, slot=sX)
                load_w(l, w_in, 0, D, C_MB + 128 * dc, 128, 128, slot=sX)
                load_w(l, w_in, 0, D, C_MC + 128 * dc, 128, 0, slot=sY)
                load_w(l, wpc, 0, 256, 128 * dc, 128, 128, slot=sY)
                load_w(l, wpa, 0, 512, 128 * dc, 128, 0, slot=sZ)
                load_w(l, wpb, 0, 256, 128 * dc, 128, 128, slot=sZ)
                for (t0, ln) in TCS:
                    fm_proj(sZ, 0, 4, (lambda k, t0, ln: br[:, k, t0:t0 + ln]), t0, ln, 0, [res("br%d" % i) for i in range(4)])
                    fm_proj(sZ, 128, 2, (lambda k, t0, ln: br[:, 4 + k, t0:t0 + ln]), t0, ln, 1, [res("br4"), res("br5")])
                    fm_proj(sY, 128, 2, (lambda k, t0, ln: br[:, 6 + k, t0:t0 + ln]), t0, ln, 2, [res("br6"), res("br7")])
                    fm_proj(sX, 0, 8, hT_rhs, t0, ln, 3, [Rh])
                    fm_proj(sX, 128, 8, hT_rhs, t0, ln, 4, [Rh])
                    fm_proj(sY, 0, 8, hT_rhs, t0, ln, 5, [Rh])
                    for i, tt_ in enumerate((tA, tB, tC)):
                        P.op("act", lambda a, i=i, tt_=tt_: a.activation(out=tt_[:, 0:ln], in_=banks[3 + i][:, 0:ln], func=AF.Tanh, scale=0.5),
                             [Rb[3 + i]], [res("t" + "ABC"[i])])
                        P.op("dve", lambda v, i=i, tt_=tt_: v.scalar_tensor_tensor(out=tt_[:, 0:ln], in0=tt_[:, 0:ln], scalar=1.0, in1=banks[i][:, 0:ln],
                                                                                 op0=ALU.add, op1=ALU.mult), [res("t" + "ABC"[i]), Rb[i]], [res("t" + "ABC"[i])])
                    P.op("dve", lambda v: v.tensor_tensor(out=tA[:, 0:ln], in0=tA[:, 0:ln], in1=tB[:, 0:ln], op=ALU.add), [res("tA"), res("tB")], [res("tA")])
                    P.op("dve", lambda v: v.tensor_tensor(out=mT[:, dc, t0:t0 + ln], in0=tA[:, 0:ln], in1=tC[:, 0:ln], op=ALU.add), [res("tA"), res("tC")], [Rm])
            for q in range(4):
                so = load_w(l, w_out, 0, D, 256 * q, 256)
                for tt in range(17):
                    rows = 128 if tt < 16 else TS
                    t0 = tt * 128
                    b = next_pbank()
                    fns = [lambda t, k=k: t.matmul(banks[b][0:rows, 0:256], lhsT=mT[:, k, t0:t0 + rows], rhs=wring[so][:, k, 0:256],
                                                   start=(k == 0), stop=(k == 7)) for k in range(8)]
                    P.group(fns, reads=[Rw[so], Rm], writes=[Rb[b]])
                    P.op("dve", lambda v: v.scalar_tensor_tensor(out=x[0:rows, tt, 256 * q:256 * q + 256], in0=banks[b][0:rows, 0:256], scalar=0.5,
                                                                 in1=x[0:rows, tt, 256 * q:256 * q + 256], op0=ALU.mult, op1=ALU.add), [Rb[b], Rx[tt]], [Rx[tt]])

        try:
          for l in range(DEPTH):
              ckpt('start')
              state_copies(l)
              if l > 0:
                  P.op("pool", lambda g: g.memset(Vpad[:, :, 0, 64:128], 0.0), writes=[res("Vpad")])
                  P.op("pool", lambda g: g.memset(Vpad[:, :, 1, 0:64], 0.0), writes=[res("Vpad")])
              P.dma("sp", None, nwt[:], nw[l, :, :], writes=[res("nwt")])
              P.dma("sp", None, sinkt[:], sinkx[l, :, :], writes=[res("sinkt")])
              P.dma("sp", None, pst[:], pscale[l, :, :], writes=[res("pst")])
              P.op("act", lambda a: a.activation(out=sinkt[:], in_=sinkt[:], func=AF.Exp), reads=[res("sinkt")], writes=[res("sinkt")])
              rmsnorm_stats()
              for tt in range(17):
                  rows = 128 if tt < 16 else TS
                  t0 = tt * 128
                  P.op("act", lambda a: a.activation(out=hb[0:rows, :], in_=x[0:rows, tt, :], func=AF.Copy, scale=rstd[0:rows, tt:tt + 1]),
                       reads=[Rx[tt], res("rstd")], writes=[res("hb")])
                  b = next_pbank()
                  pb16 = banks[b][:, :].bitcast(BF16)
                  fns = [lambda t, k=k: t.transpose(out=pb16[:, k * 128:k * 128 + rows], in_=hb[0:rows, k * 128:(k + 1) * 128],
                                                    identity=ident[0:rows, 0:rows]) for k in range(8)]
                  P.group(fns, reads=[res("hb"), res("cbf")], writes=[Rb[b]])
                  for k in range(8):
                      evac("act" if k % 2 == 0 else "dve", hT[:, k, t0:t0 + rows], pb16[:, k * 128:k * 128 + rows], [Rb[b], res("nwt")], [Rh],
                           scale=nwt[:, k:k + 1])
              ckpt('R')
              for c in range(4):
                  kvh = c // 2
                  attention(l, c, [(0, T)], 1, C_QA + 128 * c, C_KA + 64 * kvh, C_VA + 64 * kvh, True, C_GA + 128 * c, sa, 64 * kvh, 128,
                            nap, nas, 64 * kvh, (c % 2 == 0), "A", c)
                  ckpt('A%d' % c)
              for jc in range(2):
                  for gi, g in enumerate((2, 1, 0)):
                      Wg, dg = B_PAIRS[g]
                      hc = g * 256 + jc * 128
                      attention(l, 4 + g * 2 + jc, [(r, T // dg) for r in range(dg)], dg, C_QB + hc, C_KB + hc, C_VB + hc, False, C_GB + 128 * jc,
                                sb[g], jc * 128, Wg, nbp[g], nbs[g], jc * 128, True, ("B0", "B1", "B2")[gi], 4 + jc)
              ckpt('B')
              P.barrier(ALLD)
              pool_phase(l)
              ckpt('C')
              P.barrier(ALLD)
              final_phase(l)
              ckpt('F')
              P.barrier(ALLD)
        except StopBuild:
            pass
        P.dma("sp", None, fnwb, fnw[:, :], writes=[res("fnwb")])
        rmsnorm_stats()
        for tt in range(17):
            rows = 128 if tt < 16 else TS
            P.op("dve", lambda v: v.scalar_tensor_tensor(out=yb[0:rows, :], in0=x[0:rows, tt, :], scalar=rstd[0:rows, tt:tt + 1], in1=fnwb[0:rows, :],
                                                         op0=ALU.mult, op1=ALU.mult), [Rx[tt], res("rstd"), res("fnwb"), res("yb")], [res("yb")])
            dst = yp[tt * 128:(tt + 1) * 128, :] if tt < 16 else ys[:, :]
            P.dma("sp", None, dst, yb[0:rows, :], reads=[res("yb")], writes=[res("yb_dma")])
        P.wait_all("sp", P.all_dma_toks() + [(d_cp[0], d_cp[1])])
    return nc


def _host_consts(NS):
    tb = _tables()
    TS = NS * 4
    tabs = np.zeros((10, 128, 648), np.float32)
    for ch, (tp, ts, tn) in enumerate(tb):
        tabs[ch, :, 0:512] = tp
        tabs[ch, :, 512:520] = ts
        full = np.zeros((64, 16, 4, 2))
        for n in range(NS):
            full[4 * n:4 * n + 4, n, :, :] = tn
        tabs[ch, 0:64, 520:648] = full.reshape(64, 128)
    c = np.zeros((128, 512), np.float32)
    c[:, 0:128] = np.eye(128)
    c[:, 128:192] = 1.0
    c[:, 320:384] = 1.0
    t = np.arange(16)
    for cc, ws in enumerate(((2, 4), (8, 16))):
        for hi in range(2):
            c[64 * hi:64 * hi + 64, 384 + 16 * cc:400 + 16 * cc] = 1.0 / np.minimum(t + 1, ws[hi])
    return tabs, c


_CACHE = {}


def kernel(x_prompt, x_sample, state_a, state_b1, state_b2, state_b3, state_c, norm_w, final_norm_w, w_in, sink_a,
           w_proj_a, w_proj_b, w_proj_c, w_pool, pool_scale, w_out):
    f = lambda a: np.ascontiguousarray(np.asarray(a, dtype=np.float32))
    DEPTH = w_in.shape[0]
    NS = x_sample.shape[0] // NCORES
    key = (DEPTH, NS)
    if key not in _CACHE:
        _CACHE[key] = build(DEPTH, NS)
    nc = _CACHE[key]
    tabs, consts = _host_consts(NS)
    x_prompt, x_sample = f(x_prompt), f(x_sample)
    sa, sb1, sb2, sb3, sc = f(state_a), f(state_b1), f(state_b2), f(state_b3), f(state_c)
    nwl = f(np.asarray(norm_w).reshape(DEPTH, 8, 128).transpose(0, 2, 1))
    fn = f(np.broadcast_to(np.asarray(final_norm_w)[None, :], (128, D)))
    sk = np.asarray(sink_a)
    sinkx = np.zeros((DEPTH, 128, 4), np.float32)
    for c in range(4):
        sinkx[:, 0:64, c] = sk[:, 2 * c][:, None]
        sinkx[:, 64:128, c] = sk[:, 2 * c + 1][:, None]
    psl = f(np.asarray(pool_scale).reshape(DEPTH, 2, 128).transpose(0, 2, 1))
    shared = dict(nw=nwl, fnw=fn, w_in=f(w_in), sinkx=sinkx, wpa=f(w_proj_a), wpb=f(w_proj_b), wpc=f(w_proj_c), wpool=f(w_pool),
                  pscale=psl, w_out=f(w_out), tabs=tabs, consts=consts)
    in_maps = []
    for c in range(NCORES):
        n0, n1 = c * NS, (c + 1) * NS
        m = dict(shared)
        m.update(xp=x_prompt[c], xs=x_sample[n0:n1].reshape(NS * 4, D),
                 sa=f(sa[:, n0:n1].reshape(DEPTH, NS, 2, 128, 128)),
                 sb1=f(sb1[:, n0:n1].reshape(DEPTH, NS, 2, 128, 256)),
                 sb2=f(sb2[:, n0:n1].reshape(DEPTH, NS, 2, 512, 256)),
                 sb3=f(sb3[:, n0:n1].reshape(DEPTH, NS, 2, 2048, 256)),
                 sc=f(sc[:, n0:n1]))
        in_maps.append(m)
    rr = run_bass_kernel_spmd(nc, in_maps, core_ids=list(range(NCORES))).results
    B = NCORES

    def cat(name, axis, shape):
        return np.concatenate([np.asarray(r[name], np.float32)[None] if axis is None else np.asarray(r[name], np.float32) for r in rr],
                              axis=0 if axis is None else axis).reshape(shape)
    y_p = np.stack([r["yp"] for r in rr]).astype(np.float32)
    y_s = np.concatenate([np.asarray(r["ys"]).reshape(NS, 4, D) for r in rr], 0).astype(np.float32)

    def pst_(name, tail):
        return np.stack([np.asarray(r[name]) for r in rr], 1).reshape((DEPTH, B) + tail).astype(np.float32)

    def sst_(name, tail):
        return np.concatenate([np.asarray(r[name]) for r in rr], 1).reshape((DEPTH, B * NS) + tail).astype(np.float32)
    return (y_p, y_s,
            pst_("nap", (2, 128, 2, 64)), sst_("nas", (2, 128, 2, 64)),
            pst_("nb1p", (2, 128, 4, 64)), sst_("nb1s", (2, 128, 4, 64)),
            pst_("nb2p", (2, 512, 4, 64)), sst_("nb2s", (2, 512, 4, 64)),
            pst_("nb3p", (2, 2048, 4, 64)), sst_("nb3s", (2, 2048, 4, 64)),
            pst_("ncp", (15, 256)), sst_("ncs", (15, 256)))
```

```python
import numpy as np
from contextlib import ExitStack
import concourse.bass as bass
import concourse.mybir as mybir
from concourse.bass_utils import run_bass_kernel_spmd

F32 = mybir.dt.float32
BF16 = mybir.dt.bfloat16
AF = mybir.ActivationFunctionType
ALU = mybir.AluOpType

D = 1024
T = 2048
DIN = 7424
NCORES = 8
EPS = 1e-6
C_QA, C_KA, C_VA, C_GA = 0, 512, 640, 768
C_QB, C_KB, C_VB, C_GB = 1280, 2048, 2816, 3584
C_UC, C_GC, C_MA, C_MB, C_MC = 3840, 4096, 4352, 5376, 6400
B_PAIRS = ((128, 1), (512, 4), (2048, 16))


def _slopes(n):
    return 2.0 ** (-8.0 * (np.arange(n, dtype=np.float64) + 1.0) / n)


def _tables():
    sa = _slopes(8)
    sb = _slopes(12)
    k = np.arange(128)[:, None].astype(np.float64)
    q = np.arange(128)[None, :].astype(np.float64)
    out = []
    for ch in range(10):
        if ch < 4:
            cs = [sa[2 * ch], sa[2 * ch + 1]]
            dil = 1
        else:
            g, jc = (ch - 4) // 2, (ch - 4) % 2
            dil = B_PAIRS[g][1]
            cs = [sb[g * 4 + 2 * jc] * dil, sb[g * 4 + 2 * jc + 1] * dil]
        tp = np.zeros((128, 2, 2, 128))
        ts = np.zeros((128, 4, 2))
        tn = np.zeros((4, 4, 2))
        for h in range(2):
            c = cs[h]
            dd = q - k
            tp[:, h, 0, :] = np.where(dd >= 0, np.exp(-c * np.maximum(dd, 0)), 0.0)
            do = 128 + q - k
            tp[:, h, 1, :] = np.where(do <= 128, np.exp(-c * np.minimum(do, 128.0)), 0.0)
            for j in range(4):
                m = np.arange(128)
                if dil == 1:
                    st = 128 + j - m
                    ts[:, j, h] = np.where(m >= j, np.exp(-c * st), 0.0)
                    for i in range(4):
                        tn[i, j, h] = np.exp(-c * (j - i)) if i <= j else 0.0
                else:
                    ts[:, j, h] = np.exp(-c * (128 - m))
                    tn[j, j, h] = 1.0
        out.append((tp.reshape(128, 512), ts.reshape(128, 8), tn))
    return out


class Prog:
    def __init__(self, nc, es):
        self.nc = nc
        self.es = es
        self.eng = {"pe": nc.tensor, "act": nc.scalar, "dve": nc.vector, "pool": nc.gpsimd, "sp": nc.sync}
        self.sem = {e: es.enter_context(nc.semaphore("c_" + e)) for e in self.eng}
        self.cnt = {e: 0 for e in self.eng}
        self.waited = {e: {} for e in self.eng}
        self.semobj = {}
        self.ndsem = 0

    def dsem(self, name):
        s = self.es.enter_context(self.nc.semaphore("d_" + name))
        self.ndsem += 1
        return [s, 0]

    def _wait(self, e, toks):
        best = {}
        for (s, v) in toks:
            k = id(s)
            self.semobj[k] = s
            if v > best.get(k, 0):
                best[k] = v
        for k, v in best.items():
            if self.waited[e].get(k, 0) < v:
                self.eng[e].wait_ge(self.semobj[k], v)
                self.waited[e][k] = v

    def _deps(self, e, reads, writes):
        toks = []
        mysem = self.sem[e]
        skip = (e == "pe")
        for r in reads:
            if r.w is not None:
                if r.w[0] is mysem and skip:
                    continue
                toks.append(r.w)
        for w in writes:
            if w.w is not None and not (w.w[0] is mysem and skip):
                toks.append(w.w)
            for t in w.r.values():
                if not (t[0] is mysem and skip):
                    toks.append(t)
        return toks

    def _mark(self, tok, reads, writes):
        for w in writes:
            w.w = tok
            w.r = {}
        for r in reads:
            k = id(tok[0])
            if k not in r.r or r.r[k][1] < tok[1]:
                r.r[k] = tok

    def op(self, e, fn, reads=(), writes=()):
        self._wait(e, self._deps(e, reads, writes))
        ins = fn(self.eng[e])
        self.cnt[e] += 1
        ins.then_inc(self.sem[e], 1)
        self._mark((self.sem[e], self.cnt[e]), reads, writes)

    def group(self, fns, reads=(), writes=()):
        e = "pe"
        self._wait(e, self._deps(e, reads, writes))
        ins = None
        for fn in fns:
            ins = fn(self.eng[e])
        self.cnt[e] += 1
        ins.then_inc(self.sem[e], 1)
        self._mark((self.sem[e], self.cnt[e]), reads, writes)

    def dma(self, q, ds, out, in_, reads=(), writes=()):
        toks = self._deps(q, reads, writes)
        if ds is None:
            ring = self.ring[q]
            ds = ring[self.ridx[q] % len(ring)]
            self.ridx[q] += 1
            if ds[1] > 0:
                toks.append((ds[0], ds[1]))
        self._wait(q, toks)
        ins = self.eng[q].dma_start(out=out, in_=in_)
        ds[1] += 16
        ins.then_inc(ds[0], 16)
        self._mark((ds[0], ds[1]), reads, writes)

    def mkrings(self, nsp, npool):
        self.ring = {"sp": [self.dsem("rs%d" % i) for i in range(nsp)], "pool": [self.dsem("rp%d" % i) for i in range(npool)]}
        self.ridx = {"sp": 0, "pool": 0}

    def all_dma_toks(self):
        return [(d[0], d[1]) for q in self.ring for d in self.ring[q] if d[1] > 0]

    def wait_all(self, e, toks):
        self._wait(e, toks)

    def barrier(self, dsems):
        toks = [(self.sem[f], self.cnt[f]) for f in self.eng if self.cnt[f] > 0]
        toks += [(d[0], d[1]) for d in dsems if d[1] > 0]
        toks += self.all_dma_toks()
        for e in self.eng:
            self._wait(e, toks)


def mkap(ref, dims):
    return bass.AP(ref.tensor, ref.offset, [list(ref.ap[0])] + [list(d) for d in dims])


class StopBuild(Exception):
    pass


STOP = None


def ckpt(name):
    if STOP == name:
        raise StopBuild()


class Res:
    __slots__ = ("w", "r", "n")

    def __init__(self, n=""):
        self.w = None
        self.r = {}
        self.n = n


def build(DEPTH, NS):
    TS = NS * 4
    TT = T + TS
    nc = bass.Bass("TRN2", target_bir_lowering=False)

    def din(name, shape):
        return nc.dram_tensor(name, list(shape), F32, kind="ExternalInput").ap()

    def dout(name, shape):
        return nc.dram_tensor(name, list(shape), F32, kind="ExternalOutput").ap()

    xp = din("xp", (T, D))
    xs = din("xs", (TS, D))
    sa = din("sa", (DEPTH, NS, 2, 128, 128))
    sb = [din("sb1", (DEPTH, NS, 2, 128, 256)), din("sb2", (DEPTH, NS, 2, 512, 256)),
          din("sb3", (DEPTH, NS, 2, 2048, 256))]
    sc = din("sc", (DEPTH, NS, 15, 256))
    nw = din("nw", (DEPTH, 128, 8))
    fnw = din("fnw", (128, D))
    w_in = din("w_in", (DEPTH, D, DIN))
    sinkx = din("sinkx", (DEPTH, 128, 4))
    wpa = din("wpa", (DEPTH, 512, D))
    wpb = din("wpb", (DEPTH, 256, D))
    wpc = din("wpc", (DEPTH, 256, D))
    wpool = din("wpool", (DEPTH, 4, 64, 64))
    pscale = din("pscale", (DEPTH, 128, 2))
    w_out = din("w_out", (DEPTH, D, D))
    tabs = din("tabs", (10, 128, 648))
    consts = din("consts", (128, 512))

    yp = dout("yp", (T, D))
    ys = dout("ys", (TS, D))
    nap = dout("nap", (DEPTH, 2, 128, 128))
    nas = dout("nas", (DEPTH, NS, 2, 128, 128))
    nbp = [dout("nb1p", (DEPTH, 2, 128, 256)), dout("nb2p", (DEPTH, 2, 512, 256)), dout("nb3p", (DEPTH, 2, 2048, 256))]
    nbs = [dout("nb1s", (DEPTH, NS, 2, 128, 256)), dout("nb2s", (DEPTH, NS, 2, 512, 256)),
           dout("nb3s", (DEPTH, NS, 2, 2048, 256))]
    ncp = dout("ncp", (DEPTH, 15, 256))
    ncs = dout("ncs", (DEPTH, NS, 15, 256))

    es = ExitStack()
    with es:
        def sb_t(name, shape, dt):
            return es.enter_context(nc.sbuf_tensor(name, list(shape), dt))

        P = Prog(nc, es)
        x = sb_t("x", (128, 17, D), F32)
        hT = sb_t("hT", (128, 8, TT), BF16)
        br = sb_t("br", (128, 8, TT), BF16)
        NSB = 2 if NS >= 2 else 1
        ATT_BYTES = 2 * ((TT * 2 + 3) // 4 * 4) + 17 * 2 * 128 * 2 + 2 * TT * 4 + 2048 + 3072 + 3 * 2048 + (NSB * 4 * 128 * 2) * 3 + 64 * 4 * 2
        SCR = max(ATT_BYTES, 8 * TT * 2, 3 * 2400 * 4 + TT * 2)
        scr = sb_t("scr", (128, SCR // 4), F32)
        hb = sb_t("hb", (128, D), BF16)
        wring = [sb_t("wr%d" % i, (128, 8, 256), BF16) for i in range(3)]
        stage = sb_t("stage", (128, 2, 256), F32)
        tab = sb_t("tab", (128, 648), F32)
        cst = sb_t("cst", (128, 160), F32)
        cbf = sb_t("cbf", (128, 384), BF16)
        small = sb_t("small", (128, 64), F32)
        nwt = sb_t("nwt", (128, 8), F32)
        sinkt = sb_t("sinkt", (128, 4), F32)
        pst = sb_t("pst", (128, 2), F32)
        wpl = sb_t("wpl", (128, 2, 128), BF16)
        qblk = sb_t("qblk", (128, 64, 2), BF16)
        VS = sb_t("VS", (128, 128), BF16)
        onesAllT = sb_t("onesAllT", (128, 128), BF16)
        psall = es.enter_context(nc.psum_tensor("psall", [128, 4096], F32))
        banks = [psall[:, i * 512:(i + 1) * 512] for i in range(8)]

        off = [0]

        def view(nbytes, dt, shape):
            esz = 4 if dt == F32 else 2
            o = off[0]
            off[0] += (nbytes + 3) // 4 * 4
            v = scr[:, o // 4:(o + nbytes + 3) // 4]
            if dt != F32:
                v = v.bitcast(dt)
            v = v[:, 0:nbytes // esz]
            if len(shape) == 3:
                v = v.rearrange("p (a b) -> p a b", a=shape[1])
            elif len(shape) == 4:
                v = v.rearrange("p (a b c) -> p a b c", a=shape[1], b=shape[2])
            return v
        QT = view(TT * 2, BF16, (128, TT))
        KT = view(TT * 2, BF16, (128, TT))
        Vpad = view(17 * 2 * 128 * 2, BF16, (128, 17, 2, 128))
        UZ = view(2 * TT * 4, F32, (128, 2, TT))
        Ebuf = view(2048, F32, (128, 512))
        PT = view(3072, BF16, (128, 3, 512))
        off_tA = off[0]
        tA = view(2048, F32, (128, 512))
        tB = view(2048, F32, (128, 512))
        tC = view(2048, F32, (128, 512))
        NTL = NSB * 4
        Kt = view(NTL * 128 * 2, BF16, (128, NTL, 128))
        Vt = view(NTL * 128 * 2, BF16, (128, NTL, 128))
        KTs = view(NTL * 128 * 2, BF16, (128, NTL, 128))
        Pbuf = view(64 * 4 * 2, BF16, (128, 2, 128))
        assert off[0] <= SCR, (off[0], SCR)
        mT = scr[:, 0:8 * TT // 2].bitcast(BF16).rearrange("p (a b) -> p a b", a=8)
        LB = 2400
        Ub = scr[:, 0:LB]
        Sa = scr[:, LB:2 * LB]
        Sb_ = scr[:, 2 * LB:3 * LB]
        diffT = scr[:, 3 * LB:3 * LB + TT // 2].bitcast(BF16)
        yb = scr[:, off_tA // 4:off_tA // 4 + 1024]
        fnwb = scr[:, 0:1024]

        R = {}

        def res(n):
            if n not in R:
                R[n] = Res(n)
            return R[n]
        Rb = [res("bank%d" % i) for i in range(8)]
        Rx = [res("x%d" % i) for i in range(17)]
        Rw = [res("w%d" % i) for i in range(3)]
        P.mkrings(24, 16)
        d_cp = P.dsem("cp")
        wctr = [0]

        ident = cbf[:, 0:128]
        onesE = cbf[:, 128:256]
        onesO = cbf[:, 256:384]

        def load_w(l, src, r0, nrows, c0, ncols, dcol=0, slot=None, kch=None):
            if slot is None:
                slot = wctr[0] % 3
                wctr[0] += 1
            kch = nrows // 128
            s = src[l, r0:r0 + nrows, c0:c0 + ncols].rearrange("(k p) c -> p k c", p=128)
            P.dma("pool", None, wring[slot][:, 0:kch, dcol:dcol + ncols], s, writes=[Rw[slot]])
            return slot

        def evac(e, out, in_, reads, writes, scale=None, func=None):
            if e == "act":
                kw = {}
                if scale is not None:
                    kw["scale"] = scale
                P.op("act", lambda a: a.activation(out=out, in_=in_, func=func or AF.Copy, **kw), reads, writes)
            else:
                if scale is not None:
                    P.op("dve", lambda v: v.tensor_scalar(out=out, in0=in_, scalar1=scale, scalar2=None, op0=ALU.mult), reads, writes)
                else:
                    P.op("dve", lambda v: v.tensor_copy(out=out, in_=in_), reads, writes)

        TCS = [(0, 512), (512, 512), (1024, 512), (1536, 512), (T, TS)]
        pbank = [0]

        def next_pbank():
            b = 6 + (pbank[0] % 2)
            pbank[0] += 1
            return b

        def fm_proj(slot, c0, kch, rhs_fn, t0, ln, b, extra_reads=()):
            fns = []
            for k in range(kch):
                fns.append(lambda t, k=k: t.matmul(banks[b][:, 0:ln], lhsT=wring[slot][:, k, c0:c0 + 128],
                                                   rhs=rhs_fn(k, t0, ln), start=(k == 0), stop=(k == kch - 1)))
            P.group(fns, reads=[Rw[slot]] + list(extra_reads), writes=[Rb[b]])

        def hT_rhs(k, t0, ln):
            return hT[:, k, t0:t0 + ln]

        Rh = res("hT")

        P.dma("sp", None, cst[:, 0:128], consts[:, 0:128], writes=[res("cst")])
        P.dma("sp", None, cst[:, 128:160], consts[:, 384:416], writes=[res("cst")])
        P.dma("pool", None, cbf[:], consts[:, 0:384], writes=[res("cbf")])
        for tt in range(16):
            P.dma("sp", None, x[:, tt, :], xp[tt * 128:(tt + 1) * 128, :], writes=[Rx[tt]])
        P.dma("sp", None, x[0:TS, 16, :], xs[:, :], writes=[Rx[16]])
        P.op("pool", lambda g: g.memset(qblk[:], 0.0), writes=[res("qblk")])
        pending = []

        def cp(dst, src):
            pending.append(lambda: P.dma("sp", d_cp, dst, src))

        for l in range(DEPTH):
            for (src, dst, W) in [(sa, nas, 128), (sb[0], nbs[0], 128)]:
                cp(dst[l, :, :, 0:W - 4, :].rearrange("n a r c -> (n a) (r c)"), src[l, :, :, 4:W, :].rearrange("n a r c -> (n a) (r c)"))
            cp(ncs[l, :, 0:11, :].rearrange("n r c -> n (r c)"), sc[l, :, 4:15, :].rearrange("n r c -> n (r c)"))
            for n in range(0, NS, 4):
                n1 = min(NS, n + 4)
                cp(nbs[1][l, n:n1, :, 0:508, :].rearrange("n a r c -> (n a) (r c)"), sb[1][l, n:n1, :, 4:512, :].rearrange("n a r c -> (n a) (r c)"))
            for n in range(NS):
                cp(nbs[2][l, n, :, 0:2044, :].rearrange("a r c -> a (r c)"), sb[2][l, n, :, 4:2048, :].rearrange("a r c -> a (r c)"))

        def issue_copies(k):
            for _ in range(k):
                if pending:
                    pending.pop(0)()
        ssq = small[:, 0:17]
        rstd = small[:, 17:34]
        mhalf = small[:, 34:51]
        P.op("dve", lambda v: v.memset(mhalf, -0.5), writes=[res("mhalf")])
        P.op("dve", lambda v: v.memset(Vpad[:, :, :, :], 0.0), writes=[res("Vpad")])
        P.op("dve", lambda v: v.memset(ssq, 1.0), writes=[res("ssq")])

        def rmsnorm_stats():
            for tt in range(17):
                rows = 128 if tt < 16 else TS
                P.op("act", lambda a: a.activation(out=hb[0:rows, :], in_=x[0:rows, tt, :], func=AF.Square, accum_out=ssq[0:rows, tt:tt + 1]),
                     reads=[Rx[tt]], writes=[res("hb"), res("ssq")])
            P.op("dve", lambda v: v.tensor_scalar(out=rstd, in0=ssq, scalar1=1.0 / D, scalar2=EPS, op0=ALU.mult, op1=ALU.add),
                 reads=[res("ssq")], writes=[res("rstd")])
            P.op("pool", lambda g: g.tensor_tensor(out=rstd, in0=rstd, in1=mhalf, op=ALU.pow),
                 reads=[res("rstd"), res("mhalf")], writes=[res("rstd")])
        ALLD = []
        P.op("pool", lambda g: g.memset(onesAllT[:], 1.0), writes=[res("onesAll")])
        stg = [0]

        def tm_mm(slot, c0, ncols, tok, rows):
            b = next_pbank()
            fns = []
            for k in range(8):
                fns.append(lambda t, k=k: t.matmul(banks[b][0:rows, 0:ncols], lhsT=tok(k),
                                                   rhs=wring[slot][:, k, c0:c0 + ncols], start=(k == 0), stop=(k == 7)))
            P.group(fns, reads=[Rw[slot], Rh], writes=[Rb[b]])
            return b

        def stage_out(b, rows, ncols, dsts, r0=0):
            s = stg[0] % 2
            stg[0] += 1
            rs = res("stage%d" % s)
            evac("dve", stage[0:rows, s, 0:ncols], banks[b][0:rows, 0:ncols], [Rb[b]], [rs])
            for dd in dsts:
                a0, a1, dst = dd[0:3]
                c0, c1 = (dd[3], dd[4]) if len(dd) > 3 else (0, ncols)
                P.dma("sp", None, dst, stage[a0:a1, s, c0:c1], reads=[rs])

        def gate(l, sg, c0, t0, ln):
            b = next_pbank()
            fm_proj(sg, c0, 8, hT_rhs, t0, ln, b, [Rh])
            P.op("act", lambda a: a.activation(out=tB[:, 0:ln], in_=banks[b][:, 0:ln], func=AF.Tanh, scale=0.5), [Rb[b]], [res("tB")])
            P.op("dve", lambda v: v.scalar_tensor_tensor(out=tA[:, 0:ln], in0=tB[:, 0:ln], scalar=1.0, in1=banks[b][:, 0:ln],
                                                         op0=ALU.add, op1=ALU.mult), [res("tB"), Rb[b]], [res("tA")])

        def uz_dst(which, it, d):
            c0, r, i0 = it
            if d == 1:
                return UZ[:, which, i0:i0 + 128]
            base = UZ[:, which, r + d * i0:r + d * i0 + 1]
            return mkap(base, [[d, 128]])

        def finalize(l, cur, d, mode, sg, br_k):
            bu, bz, lst = cur
            if mode in ("B0", "B1"):
                for it in lst:
                    c0 = it[0]
                    for which, bk in ((0, bu), (1, bz)):
                        dst = uz_dst(which, it, d)
                        if mode == "B0":
                            evac("act" if which == 0 else "dve", dst, banks[bk][:, c0:c0 + 128], [Rb[bk]], [res("UZ")])
                        else:
                            P.op("dve", lambda v, dst=dst, bk=bk, c0=c0: v.tensor_tensor(out=dst, in0=dst, in1=banks[bk][:, c0:c0 + 128], op=ALU.add),
                                 [Rb[bk], res("UZ")], [res("UZ")])
                return
            t0 = lst[0][2]
            gate(l, sg, 0, t0, 512)
            if mode == "A":
                P.op("dve", lambda v: v.tensor_scalar(out=tB[:, :], in0=banks[bz][:, :], scalar1=sinkt[:, br_k:br_k + 1], scalar2=None, op0=ALU.add),
                     [Rb[bz], res("sinkt"), res("tB")], [res("tB")])
                P.op("dve", lambda v: v.reciprocal(out=tB[:, :], in_=tB[:, :]), [res("tB")], [res("tB")])
                P.op("dve", lambda v: v.scalar_tensor_tensor(out=tC[:, :], in0=banks[bu][:, :], scalar=0.5, in1=tB[:, :], op0=ALU.mult, op1=ALU.mult),
                     [Rb[bu], res("tB")], [res("tC")])
            else:
                P.op("dve", lambda v: v.tensor_tensor(out=tB[:, :], in0=UZ[:, 1, t0:t0 + 512], in1=banks[bz][:, :], op=ALU.add),
                     [Rb[bz], res("UZ"), res("tB")], [res("tB")])
                P.op("dve", lambda v: v.reciprocal(out=tB[:, :], in_=tB[:, :]), [res("tB")], [res("tB")])
                P.op("dve", lambda v: v.tensor_tensor(out=tC[:, :], in0=UZ[:, 0, t0:t0 + 512], in1=banks[bu][:, :], op=ALU.add),
                     [Rb[bu], res("UZ")], [res("tC")])
                P.op("dve", lambda v: v.scalar_tensor_tensor(out=tC[:, :], in0=tC[:, :], scalar=0.5, in1=tB[:, :], op0=ALU.mult, op1=ALU.mult),
                     [res("tC"), res("tB")], [res("tC")])
            P.op("dve", lambda v: v.tensor_tensor(out=br[:, br_k, t0:t0 + 512], in0=tC[:, :], in1=tA[:, :], op=ALU.mult),
                 [res("tC"), res("tA")], [res("br%d" % br_k)])

        def sample_attention(l, d, state, st_c0, st_dup, mode, sg, br_k):
            qs = QT[:, T:T + TS]
            P.op("dve", lambda v: v.tensor_copy(out=qblk[0:64, 0:TS, 0], in_=qs[0:64, :]), [res("QT")], [res("qblk")])
            P.op("dve", lambda v: v.tensor_copy(out=qblk[64:128, 0:TS, 1], in_=qs[64:128, :]), [res("QT")], [res("qblk")])
            ncol = TS * 2
            bS, bN, bU, bZ = 0, 1, 2, 3
            qb2 = qblk[:, 0:TS, :].rearrange("p u h -> p (u h)")
            P.group([lambda t: t.matmul(banks[bN][0:TS, 0:ncol], lhsT=KT[:, T:T + TS], rhs=qb2, start=True, stop=True)],
                    reads=[res("KT"), res("qblk")], writes=[Rb[bN]])
            pn = Pbuf[0:TS, 1, 0:ncol]
            P.op("act", lambda a: a.activation(out=tB[0:TS, 0:ncol], in_=banks[bN][0:TS, 0:ncol], func=AF.Exp), [Rb[bN]], [res("tB")])
            P.op("dve", lambda v: v.tensor_tensor(out=pn, in0=tB[0:TS, 0:ncol], in1=tab[0:TS, 520:520 + ncol], op=ALU.mult),
                 [res("tB"), res("tab")], [res("Pn")])
            npn = 1 if d == 1 else 4
            for n0 in range(0, NS, NSB):
                nn = min(NSB, NS - n0)
                rowst = state.shape[-1]
                wcols = 64 if st_dup else 128
                for kv, dstt, rn in ((0, Kt, "Kt"), (1, Vt, "Vt")):
                    if npn == 1:
                        b0 = state[l, n0, kv, 0:1, st_c0:st_c0 + 1]
                        src = bass.AP(b0.tensor, b0.offset, [[rowst, 128], [2 * state.shape[3] * rowst, nn], [1, wcols]])
                        for half in range(2 if st_dup else 1):
                            dd = mkap(dstt[:, 0:1, half * 64:half * 64 + 1], [[128, nn], [1, wcols]])
                            P.dma("pool", None, dd, src, writes=[res(rn)])
                    else:
                        for n in range(n0, n0 + nn):
                            b0 = state[l, n, kv, 0:1, st_c0:st_c0 + 1]
                            src = bass.AP(b0.tensor, b0.offset, [[d * rowst, 128], [rowst, npn], [1, wcols]])
                            dd = mkap(dstt[:, (n - n0) * npn:(n - n0) * npn + 1, 0:1], [[128, npn], [1, wcols]])
                            P.dma("pool", None, dd, src, writes=[res(rn)])
                ntl = nn * npn
                b = next_pbank()
                pb16 = banks[b][:, :].bitcast(BF16)
                fns = []
                for ti in range(ntl):
                    fns.append(lambda t, ti=ti: t.transpose(out=pb16[:, ti * 128:(ti + 1) * 128], in_=Kt[:, ti, :], identity=ident))
                P.group(fns, reads=[res("Kt"), res("cbf")], writes=[Rb[b]])
                evac("dve", KTs[:, 0:ntl, :].rearrange("p a b -> p (a b)"), pb16[:, 0:ntl * 128], [Rb[b]], [res("KTs")])
                cols = []
                for n in range(n0, n0 + nn):
                    for jt in range(npn):
                        ti = (n - n0) * npn + jt
                        cols.append((ti, n * 8, 8) if d == 1 else (ti, (n * 4 + jt) * 2, 2))
                fns = [lambda t, ti=ti, cc0=cc0, cn=cn: t.matmul(banks[bS][:, cc0:cc0 + cn], lhsT=KTs[:, ti, :], rhs=qb2[:, cc0:cc0 + cn],
                                                                 start=True, stop=True, skip_group_check=True) for (ti, cc0, cn) in cols]
                P.group(fns, reads=[res("KTs"), res("qblk")], writes=[Rb[bS]])
                c_lo, c_hi = n0 * 8, (n0 + nn) * 8
                P.op("act", lambda a, c_lo=c_lo, c_hi=c_hi: a.activation(out=tC[:, c_lo:c_hi], in_=banks[bS][:, c_lo:c_hi], func=AF.Exp), [Rb[bS]], [res("tC")])
                tsb = tab[:, 512:520]
                tsb3 = mkap(tsb, [[0, nn], [1, 8]])
                P.op("dve", lambda v, c_lo=c_lo, c_hi=c_hi, tsb3=tsb3, nn=nn: v.tensor_tensor(
                    out=Pbuf[:, 0, c_lo:c_hi].rearrange("p (n c) -> p n c", n=nn), in0=tC[:, c_lo:c_hi].rearrange("p (n c) -> p n c", n=nn),
                    in1=tsb3, op=ALU.mult), [res("tC"), res("tab")], [res("Pb")])
                fu, fz = [], []
                if n0 == 0:
                    fu.append(lambda t: t.matmul(banks[bU][:, 0:ncol], lhsT=VS[0:TS, :], rhs=pn, start=True, stop=False, skip_group_check=True))
                    fz.append(lambda t: t.matmul(banks[bZ][:, 0:ncol], lhsT=onesAllT[0:TS, :], rhs=pn, start=True, stop=False, skip_group_check=True))
                for (ti, cc0, cn) in cols:
                    fu.append(lambda t, ti=ti, cc0=cc0, cn=cn: t.matmul(banks[bU][:, cc0:cc0 + cn], lhsT=Vt[:, ti, :], rhs=Pbuf[:, 0, cc0:cc0 + cn],
                                                                          start=False, stop=True, skip_group_check=True))
                fz.append(lambda t, c_lo=c_lo, c_hi=c_hi: t.matmul(banks[bZ][:, c_lo:c_hi], lhsT=onesAllT[:, :], rhs=Pbuf[:, 0, c_lo:c_hi],
                                                                     start=False, stop=True, skip_group_check=True))
                P.group(fu, reads=[res("Vt"), res("Pb"), res("Pn"), res("VS")], writes=[Rb[bU]])
                P.group(fz, reads=[res("Pb"), res("Pn"), res("onesAll")], writes=[Rb[bZ]])
            u3 = banks[bU][:, 0:ncol].rearrange("p (u h) -> p u h", h=2)
            z3 = banks[bZ][:, 0:ncol].rearrange("p (u h) -> p u h", h=2)
            halves = ((0, 64, 0), (64, 128, 1))
            if mode in ("B0", "B1"):
                for (p0, p1, h) in halves:
                    for which, src in ((0, u3), (1, z3)):
                        dst = UZ[p0:p1, which, T:T + TS]
                        bk = bU if which == 0 else bZ
                        if mode == "B0":
                            evac("dve", dst, src[p0:p1, :, h], [Rb[bk]], [res("UZ")])
                        else:
                            P.op("dve", lambda v, dst=dst, src=src, p0=p0, p1=p1, h=h: v.tensor_tensor(out=dst, in0=dst, in1=src[p0:p1, :, h], op=ALU.add),
                                 [Rb[bk], res("UZ")], [res("UZ")])
                return
            gate(l, sg, 0, T, TS)
            for (p0, p1, h) in halves:
                if mode == "A":
                    P.op("dve", lambda v, p0=p0, p1=p1, h=h: v.tensor_scalar(out=tB[p0:p1, 0:TS], in0=z3[p0:p1, :, h], scalar1=sinkt[p0:p1, br_k:br_k + 1],
                                                                            scalar2=None, op0=ALU.add), [Rb[bZ], res("sinkt"), res("tB")], [res("tB")])
                    P.op("dve", lambda v, p0=p0, p1=p1: v.reciprocal(out=tB[p0:p1, 0:TS], in_=tB[p0:p1, 0:TS]), [res("tB")], [res("tB")])
                    P.op("dve", lambda v, p0=p0, p1=p1, h=h: v.scalar_tensor_tensor(out=tC[p0:p1, 0:TS], in0=u3[p0:p1, :, h], scalar=0.5, in1=tB[p0:p1, 0:TS],
                                                                                   op0=ALU.mult, op1=ALU.mult), [Rb[bU], res("tB"), res("tC")], [res("tC")])
                else:
                    P.op("dve", lambda v, p0=p0, p1=p1, h=h: v.tensor_tensor(out=tB[p0:p1, 0:TS], in0=UZ[p0:p1, 1, T:T + TS], in1=z3[p0:p1, :, h], op=ALU.add),
                         [Rb[bZ], res("UZ"), res("tB")], [res("tB")])
                    P.op("dve", lambda v, p0=p0, p1=p1: v.reciprocal(out=tB[p0:p1, 0:TS], in_=tB[p0:p1, 0:TS]), [res("tB")], [res("tB")])
                    P.op("dve", lambda v, p0=p0, p1=p1, h=h: v.tensor_tensor(out=tC[p0:p1, 0:TS], in0=UZ[p0:p1, 0, T:T + TS], in1=u3[p0:p1, :, h], op=ALU.add),
                         [Rb[bU], res("UZ"), res("tC")], [res("tC")])
                    P.op("dve", lambda v, p0=p0, p1=p1: v.scalar_tensor_tensor(out=tC[p0:p1, 0:TS], in0=tC[p0:p1, 0:TS], scalar=0.5, in1=tB[p0:p1, 0:TS],
                                                                             op0=ALU.mult, op1=ALU.mult), [res("tC"), res("tB")], [res("tC")])
            P.op("dve", lambda v: v.tensor_tensor(out=br[:, br_k, T:T + TS], in0=tC[:, 0:TS], in1=tA[:, 0:TS], op=ALU.mult),
                 [res("tC"), res("tA")], [res("br%d" % br_k)])

        def attention(l, ch, seqs, d, q_c0, k_c0, v_c0, dup, g_c0, state, st_c0, W, kout, kouts, out_c0, do_out, mode, br_k):
            P.dma("sp", None, tab[:], tabs[ch, :, :], writes=[res("tab")])
            L = T // d
            sq = load_w(l, w_in, 0, D, q_c0, 128)
            sk = wctr[0] % 3
            wctr[0] += 1
            if dup:
                for dc_, c_ in ((0, k_c0), (64, k_c0), (128, v_c0), (192, v_c0)):
                    load_w(l, w_in, 0, D, c_, 64, dc_, slot=sk)
            else:
                load_w(l, w_in, 0, D, k_c0, 128, 0, slot=sk)
                load_w(l, w_in, 0, D, v_c0, 128, 128, slot=sk)
            for (dst, rn, slot, scl, e) in [(QT, "QT", sq, 0.125, "act"), (KT, "KT", sk, None, "dve")]:
                for (t0, ln) in TCS:
                    b = next_pbank()
                    fm_proj(slot, 0, 8, hT_rhs, t0, ln, b, [Rh])
                    if t0 < T and d > 1:
                        src = banks[b][:, 0:ln].rearrange("p (a r) -> p r a", r=d)
                        o1 = dst[:, t0 // d:t0 // d + 1]
                        dd = mkap(o1, [[L, d], [1, ln // d]])
                    else:
                        src = banks[b][:, 0:ln]
                        dd = dst[:, t0:t0 + ln]
                    evac(e, dd, src, [Rb[b]], [res(rn)], scale=scl)
            ckpt('att1')
            ckpt('att1' + mode)
            tiles = []
            for (r, Ls) in seqs:
                for bb in range(Ls // 128):
                    tiles.append((r, bb))
            nv = 64 if dup else 128
            for ti in range(17):
                if ti < 16:
                    r, bb = tiles[ti]
                    start = r + d * 128 * bb
                    rows = 128
                    tok = (lambda k, start=start: hT[:, k, start:start + d * 127 + 1:d])
                else:
                    rows = TS
                    tok = (lambda k: hT[:, k, T:T + TS])
                b = tm_mm(sk, 128, 128, tok, rows)
                evac("dve", Vpad[0:rows, ti, 0, 0:64], banks[b][0:rows, 0:64], [Rb[b]], [res("Vpad")])
                evac("act", Vpad[0:rows, ti, 1, 64:128], banks[b][0:rows, (0 if dup else 64):(64 if dup else 128)], [Rb[b]], [res("Vpad")])
                if ti == 16:
                    evac("act", VS[0:TS, :], banks[b][0:TS, 0:128], [Rb[b]], [res("VS")])
                    if do_out:
                        stage_out(b, TS, nv, [(4 * n, 4 * n + 4, kouts[l, n, 1, W - 4:W, out_c0:out_c0 + nv]) for n in range(NS)])
                elif do_out and d == 1 and start >= T - W:
                    s0 = start - (T - W)
                    stage_out(b, 128, nv, [(0, 128, kout[l, 1, s0:s0 + 128, out_c0:out_c0 + nv])])
            ckpt('att2')
            ckpt('att2' + mode)
            if do_out:
                for tt in range(16):
                    if tt * 128 >= T - W:
                        s0 = tt * 128 - (T - W)
                        if d > 1:
                            b = tm_mm(sk, 0, 256, (lambda k, tt=tt: hT[:, k, tt * 128:(tt + 1) * 128]), 128)
                            stage_out(b, 128, 256, [(0, 128, kout[l, 0, s0:s0 + 128, out_c0:out_c0 + 128], 0, 128),
                                                    (0, 128, kout[l, 1, s0:s0 + 128, out_c0:out_c0 + 128], 128, 256)])
                        else:
                            b = tm_mm(sk, 0, nv, (lambda k, tt=tt: hT[:, k, tt * 128:(tt + 1) * 128]), 128)
                            stage_out(b, 128, nv, [(0, 128, kout[l, 0, s0:s0 + 128, out_c0:out_c0 + nv])])
                b = tm_mm(sk, 0, nv, (lambda k: hT[:, k, T:T + TS]), TS)
                stage_out(b, TS, nv, [(4 * n, 4 * n + 4, kouts[l, n, 0, W - 4:W, out_c0:out_c0 + nv]) for n in range(NS)])
            issue_copies(3)
            ckpt('att3')
            ckpt('att3' + mode)
            sg = load_w(l, w_in, 0, D, g_c0, 128)
            RQ, RK, RV = res("QT"), res("KT"), res("Vpad")
            uzb = 0
            win_cols = 0
            cur = None
            kbg = 0
            for (r, Ls) in seqs:
                nb = Ls // 128
                base = (r * L) if d > 1 else 0
                for kb in range(nb):
                    k0 = base + kb * 128
                    nq = 256 if kb < nb - 1 else 128
                    fns = [lambda t, h=h, k0=k0, nq=nq: t.matmul(
                        banks[h][:, 0:nq], lhsT=KT[h * 64:(h + 1) * 64, k0:k0 + 128],
                        rhs=QT[h * 64:(h + 1) * 64, k0:k0 + nq], start=True, stop=True, skip_group_check=True) for h in range(2)]
                    P.group(fns, reads=[RQ, RK], writes=[Rb[0], Rb[1]])
                    s3 = psall[:, 0:1024].rearrange("p (h q) -> p h q", h=2)[:, :, 0:nq]
                    e3 = Ebuf.rearrange("p (h q) -> p h q", h=2)[:, :, 0:nq]
                    t3 = tab[:, 0:512].rearrange("p (h q) -> p h q", h=2)[:, :, 0:nq]
                    pslot = kbg % 3
                    p3 = PT[:, pslot, :].rearrange("p (h q) -> p h q", h=2)[:, :, 0:nq]
                    P.op("act", lambda a, s3=s3, e3=e3: a.activation(out=e3, in_=s3, func=AF.Exp), [Rb[0], Rb[1]], [res("E")])
                    P.op("dve", lambda v, e3=e3, t3=t3, p3=p3: v.tensor_tensor(out=p3, in0=e3, in1=t3, op=ALU.mult),
                         [res("E"), res("tab")], [res("PT%d" % pslot)])
                    if STOP == 'lp1':
                        kbg += 1
                        continue
                    if win_cols == 0:
                        cur = (2 + 2 * (uzb % 2), 3 + 2 * (uzb % 2), [])
                        uzb += 1
                    bu, bz, lst = cur
                    c0 = win_cols
                    parts = ([(kbg - 1, 128)] if kb > 0 else []) + [(kbg, 0)]
                    fu, fz = [], []
                    npart = 2 * len(parts)
                    idx = 0
                    for h in range(2):
                        for (tk, qo) in parts:
                            st_, sp_ = (idx == 0 and c0 == 0), (idx == npart - 1)
                            fu.append(lambda t, h=h, tk=tk, qo=qo, st_=st_, sp_=sp_, c0=c0, bu=bu: t.matmul(
                                banks[bu][:, c0:c0 + 128], lhsT=Vpad[:, tk, h, :], rhs=PT[:, tk % 3, h * 256 + qo:h * 256 + qo + 128],
                                start=st_, stop=sp_, skip_group_check=True))
                            fz.append(lambda t, h=h, tk=tk, qo=qo, st_=st_, sp_=sp_, c0=c0, bz=bz: t.matmul(
                                banks[bz][:, c0:c0 + 128], lhsT=(onesE if h == 0 else onesO), rhs=PT[:, tk % 3, h * 256 + qo:h * 256 + qo + 128],
                                start=st_, stop=sp_, skip_group_check=True))
                            idx += 1
                    rp = [res("PT%d" % (tk % 3)) for (tk, _) in parts]
                    P.group(fu, reads=rp + [RV], writes=[Rb[bu]])
                    P.group(fz, reads=rp + [res("cbf")], writes=[Rb[bz]])
                    lst.append((c0, r, kb * 128))
                    win_cols += 128
                    kbg += 1
                    if win_cols == 512:
                        if STOP != 'lp2':
                            finalize(l, cur, d, mode, sg, br_k)
                        win_cols = 0
            ckpt('att4')
            ckpt('att4' + mode)
            ckpt('lp1')
            ckpt('lp2')
            sample_attention(l, d, state, st_c0, dup, mode, sg, br_k)
            ckpt('end' + mode)

        def pool_phase(l):
            P.op("pool", lambda g: g.memset(wpl[:], 0.0), writes=[res("wpl")])
            for gi in range(4):
                p0 = 64 * (gi % 2)
                P.dma("pool", None, wpl[p0:p0 + 64, gi // 2, p0:p0 + 64], wpool[l, gi, :, :], writes=[res("wpl")])
            P.op("dve", lambda v: v.tensor_scalar(out=pst[:], in0=pst[:], scalar1=0.5, scalar2=None, op0=ALU.mult), [res("pst")], [res("pst")])
            s2 = load_w(l, w_in, 0, D, C_UC, 256)
            b = tm_mm(s2, 0, 256, (lambda k: hT[:, k, T - 128:T]), 128)
            stage_out(b, 128, 256, [(113, 128, ncp[l, :, :])])
            b = tm_mm(s2, 0, 256, (lambda k: hT[:, k, T:T + TS]), TS)
            stage_out(b, TS, 256, [(4 * n, 4 * n + 4, ncs[l, n, 11:15, :]) for n in range(NS)])
            SB0 = 16 + T
            NL = SB0 + 19 * NS
            for cc in range(2):
                su = load_w(l, w_in, 0, D, C_UC + 128 * cc, 128)
                sgc = load_w(l, w_in, 0, D, C_GC + 128 * cc, 128)
                RU = res("Ub")
                P.op("dve", lambda v: v.memset(Ub[:, 0:16], 0.0), writes=[RU])
                for (t0, ln) in TCS:
                    b = next_pbank()
                    fm_proj(su, 0, 8, hT_rhs, t0, ln, b, [Rh])
                    if t0 < T:
                        evac("dve", Ub[:, 16 + t0:16 + t0 + ln], banks[b][:, 0:ln], [Rb[b]], [RU])
                    else:
                        o1 = Ub[:, SB0 + 15:SB0 + 16]
                        dd = mkap(o1, [[19, NS], [1, 4]])
                        evac("dve", dd, banks[b][:, 0:TS].rearrange("p (n j) -> p n j", j=4), [Rb[b]], [RU])
                for n0 in range(0, NS, 8):
                    nn = min(8, NS - n0)
                    rows = nn * 15
                    P.dma("sp", None, tC[0:rows, 0:256], sc[l, n0:n0 + nn, :, :].rearrange("n r c -> (n r) c"), writes=[res("tC")])
                    b = next_pbank()
                    P.group([lambda t: t.transpose(out=banks[b][:, 0:rows], in_=tC[0:rows, cc * 128:(cc + 1) * 128], identity=cst[0:rows, 0:rows])],
                            reads=[res("tC"), res("cst")], writes=[Rb[b]])
                    o1 = Ub[:, SB0 + 19 * n0:SB0 + 19 * n0 + 1]
                    dd = mkap(o1, [[19, nn], [1, 15]])
                    evac("dve", dd, banks[b][:, 0:rows].rearrange("p (n r) -> p n r", r=15), [Rb[b]], [RU])
                RSa, RSb = res("Sa"), res("Sb")
                P.op("dve", lambda v: v.tensor_tensor(out=Sa[:, 1:NL], in0=Ub[:, 1:NL], in1=Ub[:, 0:NL - 1], op=ALU.add), [RU], [RSa])
                P.op("dve", lambda v: v.tensor_tensor(out=Sb_[:, 3:NL], in0=Sa[:, 3:NL], in1=Sa[:, 1:NL - 2], op=ALU.add), [RSa], [RSb])
                if cc == 1:
                    P.op("dve", lambda v: v.tensor_tensor(out=Sa[:, 7:NL], in0=Sb_[:, 7:NL], in1=Sb_[:, 3:NL - 4], op=ALU.add), [RSb, RSa], [RSa])
                    P.op("dve", lambda v: v.tensor_tensor(out=Sb_[:, 15:NL], in0=Sa[:, 15:NL], in1=Sa[:, 7:NL - 8], op=ALU.add), [RSa, RSb], [RSb])
                ws = (2, 4) if cc == 0 else (8, 16)
                RD = res("diffT")
                for hi, (S_, RS) in enumerate(((Sa, RSa), (Sb_, RSb))):
                    p0, p1 = 64 * hi, 64 * hi + 64
                    iw = 1.0 / ws[hi]
                    P.op("dve", lambda v, S_=S_, p0=p0, p1=p1, iw=iw: v.scalar_tensor_tensor(out=diffT[p0:p1, 0:T], in0=S_[p0:p1, 16:16 + T], scalar=iw,
                                                                                           in1=Ub[p0:p1, 16:16 + T], op0=ALU.mult, op1=ALU.subtract), [RS, RU], [RD])
                    o1 = S_[p0:p1, SB0 + 15:SB0 + 16]
                    sv = mkap(o1, [[19, NS], [1, 4]])
                    o2 = Ub[p0:p1, SB0 + 15:SB0 + 16]
                    uv = mkap(o2, [[19, NS], [1, 4]])
                    P.op("dve", lambda v, sv=sv, uv=uv, p0=p0, p1=p1, iw=iw: v.scalar_tensor_tensor(
                        out=diffT[p0:p1, T:T + TS].rearrange("p (n j) -> p n j", j=4), in0=sv, scalar=iw, in1=uv, op0=ALU.mult, op1=ALU.subtract), [RS, RU], [RD])
                    ic = cst[p0:p1, 128 + 16 * cc:144 + 16 * cc]
                    P.op("dve", lambda v, S_=S_, p0=p0, p1=p1, ic=ic: v.tensor_tensor(out=tC[p0:p1, 0:16], in0=S_[p0:p1, 16:32], in1=ic, op=ALU.mult),
                         [RS, res("cst"), res("tC")], [res("tC")])
                    P.op("dve", lambda v, p0=p0, p1=p1: v.tensor_tensor(out=diffT[p0:p1, 0:16], in0=tC[p0:p1, 0:16], in1=Ub[p0:p1, 16:32], op=ALU.subtract),
                         [res("tC"), RU, RD], [RD])
                for (t0, ln) in TCS:
                    b = next_pbank()
                    P.group([lambda t: t.matmul(banks[b][:, 0:ln], lhsT=wpl[:, cc, :], rhs=diffT[:, t0:t0 + ln], start=True, stop=True)],
                            reads=[res("wpl"), RD], writes=[Rb[b]])
                    gate(l, sgc, 0, t0, ln)
                    P.op("dve", lambda v: v.scalar_tensor_tensor(out=br[:, 6 + cc, t0:t0 + ln], in0=banks[b][:, 0:ln], scalar=pst[:, cc:cc + 1], in1=tA[:, 0:ln],
                                                                 op0=ALU.mult, op1=ALU.mult), [Rb[b], res("pst"), res("tA")], [res("br%d" % (6 + cc))])

        def final_phase(l):
            Rm = res("mT")
            for dc in range(8):
                sX = wctr[0] % 3
                sY = (wctr[0] + 1) % 3
                sZ = (wctr[0] + 2) % 3
                wctr[0] += 3
                load_w(l, w_in, 0, D, C_MA + 128 * dc, 128, 0, slot=sX)
                load_w(l, w_in, 0, D, C_MB + 128 * dc, 128, 128, slot=sX)
                load_w(l, w_in, 0, D, C_MC + 128 * dc, 128, 0, slot=sY)
                load_w(l, wpc, 0, 256, 128 * dc, 128, 128, slot=sY)
                load_w(l, wpa, 0, 512, 128 * dc, 128, 0, slot=sZ)
                load_w(l, wpb, 0, 256, 128 * dc, 128, 128, slot=sZ)
                for (t0, ln) in TCS:
                    fm_proj(sZ, 0, 4, (lambda k, t0, ln: br[:, k, t0:t0 + ln]), t0, ln, 0, [res("br%d" % i) for i in range(4)])
                    fm_proj(sZ, 128, 2, (lambda k, t0, ln: br[:, 4 + k, t0:t0 + ln]), t0, ln, 1, [res("br4"), res("br5")])
                    fm_proj(sY, 128, 2, (lambda k, t0, ln: br[:, 6 + k, t0:t0 + ln]), t0, ln, 2, [res("br6"), res("br7")])
                    fm_proj(sX, 0, 8, hT_rhs, t0, ln, 3, [Rh])
                    fm_proj(sX, 128, 8, hT_rhs, t0, ln, 4, [Rh])
                    fm_proj(sY, 0, 8, hT_rhs, t0, ln, 5, [Rh])
                    for i, tt_ in enumerate((tA, tB, tC)):
                        P.op("act", lambda a, i=i, tt_=tt_: a.activation(out=tt_[:, 0:ln], in_=banks[3 + i][:, 0:ln], func=AF.Tanh, scale=0.5),
                             [Rb[3 + i]], [res("t" + "ABC"[i])])
                        P.op("dve", lambda v, i=i, tt_=tt_: v.scalar_tensor_tensor(out=tt_[:, 0:ln], in0=tt_[:, 0:ln], scalar=1.0, in1=banks[i][:, 0:ln],
                                                                                 op0=ALU.add, op1=ALU.mult), [res("t" + "ABC"[i]), Rb[i]], [res("t" + "ABC"[i])])
                    P.op("dve", lambda v: v.tensor_tensor(out=tA[:, 0:ln], in0=tA[:, 0:ln], in1=tB[:, 0:ln], op=ALU.add), [res("tA"), res("tB")], [res("tA")])
                    P.op("dve", lambda v: v.tensor_tensor(out=mT[:, dc, t0:t0 + ln], in0=tA[:, 0:ln], in1=tC[:, 0:ln], op=ALU.add), [res("tA"), res("tC")], [Rm])
            for q in range(4):
                so = load_w(l, w_out, 0, D, 256 * q, 256)
                for tt in range(17):
                    rows = 128 if tt < 16 else TS
                    t0 = tt * 128
                    b = next_pbank()
                    fns = [lambda t, k=k: t.matmul(banks[b][0:rows, 0:256], lhsT=mT[:, k, t0:t0 + rows], rhs=wring[so][:, k, 0:256],
                                                   start=(k == 0), stop=(k == 7)) for k in range(8)]
                    P.group(fns, reads=[Rw[so], Rm], writes=[Rb[b]])
                    P.op("dve", lambda v: v.scalar_tensor_tensor(out=x[0:rows, tt, 256 * q:256 * q + 256], in0=banks[b][0:rows, 0:256], scalar=0.5,
                                                                 in1=x[0:rows, tt, 256 * q:256 * q + 256], op0=ALU.mult, op1=ALU.add), [Rb[b], Rx[tt]], [Rx[tt]])

        try:
          for l in range(DEPTH):
              ckpt('start')
              if l > 0:
                  P.op("pool", lambda g: g.memset(Vpad[:, :, 0, 64:128], 0.0), writes=[res("Vpad")])
                  P.op("pool", lambda g: g.memset(Vpad[:, :, 1, 0:64], 0.0), writes=[res("Vpad")])
              P.dma("sp", None, nwt[:], nw[l, :, :], writes=[res("nwt")])
              P.dma("sp", None, sinkt[:], sinkx[l, :, :], writes=[res("sinkt")])
              P.dma("sp", None, pst[:], pscale[l, :, :], writes=[res("pst")])
              P.op("act", lambda a: a.activation(out=sinkt[:], in_=sinkt[:], func=AF.Exp), reads=[res("sinkt")], writes=[res("sinkt")])
              rmsnorm_stats()
              for tt in range(17):
                  rows = 128 if tt < 16 else TS
                  t0 = tt * 128
                  P.op("act", lambda a: a.activation(out=hb[0:rows, :], in_=x[0:rows, tt, :], func=AF.Copy, scale=rstd[0:rows, tt:tt + 1]),
                       reads=[Rx[tt], res("rstd")], writes=[res("hb")])
                  b = next_pbank()
                  pb16 = banks[b][:, :].bitcast(BF16)
                  fns = [lambda t, k=k: t.transpose(out=pb16[:, k * 128:k * 128 + rows], in_=hb[0:rows, k * 128:(k + 1) * 128],
                                                    identity=ident[0:rows, 0:rows]) for k in range(8)]
                  P.group(fns, reads=[res("hb"), res("cbf")], writes=[Rb[b]])
                  for k in range(8):
                      evac("act" if k % 2 == 0 else "dve", hT[:, k, t0:t0 + rows], pb16[:, k * 128:k * 128 + rows], [Rb[b], res("nwt")], [Rh],
                           scale=nwt[:, k:k + 1])
              ckpt('R')
              for c in range(4):
                  kvh = c // 2
                  attention(l, c, [(0, T)], 1, C_QA + 128 * c, C_KA + 64 * kvh, C_VA + 64 * kvh, True, C_GA + 128 * c, sa, 64 * kvh, 128,
                            nap, nas, 64 * kvh, (c % 2 == 0), "A", c)
                  ckpt('A%d' % c)
              for jc in range(2):
                  for gi, g in enumerate((2, 1, 0)):
                      Wg, dg = B_PAIRS[g]
                      hc = g * 256 + jc * 128
                      attention(l, 4 + g * 2 + jc, [(r, T // dg) for r in range(dg)], dg, C_QB + hc, C_KB + hc, C_VB + hc, False, C_GB + 128 * jc,
                                sb[g], jc * 128, Wg, nbp[g], nbs[g], jc * 128, True, ("B0", "B1", "B2")[gi], 4 + jc)
              ckpt('B')
              P.barrier(ALLD)
              pool_phase(l)
              ckpt('C')
              P.barrier(ALLD)
              final_phase(l)
              ckpt('F')
              P.barrier(ALLD)
        except StopBuild:
            pass
        issue_copies(len(pending))
        P.dma("sp", None, fnwb, fnw[:, :], writes=[res("fnwb")])
        rmsnorm_stats()
        for tt in range(17):
            rows = 128 if tt < 16 else TS
            P.op("dve", lambda v: v.scalar_tensor_tensor(out=yb[0:rows, :], in0=x[0:rows, tt, :], scalar=rstd[0:rows, tt:tt + 1], in1=fnwb[0:rows, :],
                                                         op0=ALU.mult, op1=ALU.mult), [Rx[tt], res("rstd"), res("fnwb"), res("yb")], [res("yb")])
            dst = yp[tt * 128:(tt + 1) * 128, :] if tt < 16 else ys[:, :]
            P.dma("sp", None, dst, yb[0:rows, :], reads=[res("yb")], writes=[res("yb_dma")])
        P.wait_all("sp", P.all_dma_toks() + [(d_cp[0], d_cp[1])])
    return nc


def _host_consts(NS):
    tb = _tables()
    TS = NS * 4
    tabs = np.zeros((10, 128, 648), np.float32)
    for ch, (tp, ts, tn) in enumerate(tb):
        tabs[ch, :, 0:512] = tp
        tabs[ch, :, 512:520] = ts
        full = np.zeros((64, 16, 4, 2))
        for n in range(NS):
            full[4 * n:4 * n + 4, n, :, :] = tn
        tabs[ch, 0:64, 520:648] = full.reshape(64, 128)
    c = np.zeros((128, 512), np.float32)
    c[:, 0:128] = np.eye(128)
    c[:, 128:192] = 1.0
    c[:, 320:384] = 1.0
    t = np.arange(16)
    for cc, ws in enumerate(((2, 4), (8, 16))):
        for hi in range(2):
            c[64 * hi:64 * hi + 64, 384 + 16 * cc:400 + 16 * cc] = 1.0 / np.minimum(t + 1, ws[hi])
    return tabs, c


_CACHE = {}


def kernel(x_prompt, x_sample, state_a, state_b1, state_b2, state_b3, state_c, norm_w, final_norm_w, w_in, sink_a,
           w_proj_a, w_proj_b, w_proj_c, w_pool, pool_scale, w_out):
    f = lambda a: np.ascontiguousarray(np.asarray(a, dtype=np.float32))
    DEPTH = w_in.shape[0]
    NS = x_sample.shape[0] // NCORES
    key = (DEPTH, NS)
    if key not in _CACHE:
        _CACHE[key] = build(DEPTH, NS)
    nc = _CACHE[key]
    tabs, consts = _host_consts(NS)
    x_prompt, x_sample = f(x_prompt), f(x_sample)
    sa, sb1, sb2, sb3, sc = f(state_a), f(state_b1), f(state_b2), f(state_b3), f(state_c)
    nwl = f(np.asarray(norm_w).reshape(DEPTH, 8, 128).transpose(0, 2, 1))
    fn = f(np.broadcast_to(np.asarray(final_norm_w)[None, :], (128, D)))
    sk = np.asarray(sink_a)
    sinkx = np.zeros((DEPTH, 128, 4), np.float32)
    for c in range(4):
        sinkx[:, 0:64, c] = sk[:, 2 * c][:, None]
        sinkx[:, 64:128, c] = sk[:, 2 * c + 1][:, None]
    psl = f(np.asarray(pool_scale).reshape(DEPTH, 2, 128).transpose(0, 2, 1))
    shared = dict(nw=nwl, fnw=fn, w_in=f(w_in), sinkx=sinkx, wpa=f(w_proj_a), wpb=f(w_proj_b), wpc=f(w_proj_c), wpool=f(w_pool),
                  pscale=psl, w_out=f(w_out), tabs=tabs, consts=consts)
    in_maps = []
    for c in range(NCORES):
        n0, n1 = c * NS, (c + 1) * NS
        m = dict(shared)
        m.update(xp=x_prompt[c], xs=x_sample[n0:n1].reshape(NS * 4, D),
                 sa=f(sa[:, n0:n1].reshape(DEPTH, NS, 2, 128, 128)),
                 sb1=f(sb1[:, n0:n1].reshape(DEPTH, NS, 2, 128, 256)),
                 sb2=f(sb2[:, n0:n1].reshape(DEPTH, NS, 2, 512, 256)),
                 sb3=f(sb3[:, n0:n1].reshape(DEPTH, NS, 2, 2048, 256)),
                 sc=f(sc[:, n0:n1]))
        in_maps.append(m)
    rr = run_bass_kernel_spmd(nc, in_maps, core_ids=list(range(NCORES))).results
    B = NCORES

    def cat(name, axis, shape):
        return np.concatenate([np.asarray(r[name], np.float32)[None] if axis is None else np.asarray(r[name], np.float32) for r in rr],
                              axis=0 if axis is None else axis).reshape(shape)
    y_p = np.stack([r["yp"] for r in rr]).astype(np.float32)
    y_s = np.concatenate([np.asarray(r["ys"]).reshape(NS, 4, D) for r in rr], 0).astype(np.float32)

    def pst_(name, tail):
        return np.stack([np.asarray(r[name]) for r in rr], 1).reshape((DEPTH, B) + tail).astype(np.float32)

    def sst_(name, tail):
        return np.concatenate([np.asarray(r[name]) for r in rr], 1).reshape((DEPTH, B * NS) + tail).astype(np.float32)
    return (y_p, y_s,
            pst_("nap", (2, 128, 2, 64)), sst_("nas", (2, 128, 2, 64)),
            pst_("nb1p", (2, 128, 4, 64)), sst_("nb1s", (2, 128, 4, 64)),
            pst_("nb2p", (2, 512, 4, 64)), sst_("nb2s", (2, 512, 4, 64)),
            pst_("nb3p", (2, 2048, 4, 64)), sst_("nb3s", (2, 2048, 4, 64)),
            pst_("ncp", (15, 256)), sst_("ncs", (15, 256)))
```

```python
import numpy as np
from contextlib import ExitStack
import concourse.bass as bass
import concourse.mybir as mybir
from concourse.bass_utils import run_bass_kernel_spmd

F32 = mybir.dt.float32
BF16 = mybir.dt.bfloat16
AF = mybir.ActivationFunctionType
ALU = mybir.AluOpType

D = 1024
T = 2048
DIN = 7424
NCORES = 8
EPS = 1e-6
C_QA, C_KA, C_VA, C_GA = 0, 512, 640, 768
C_QB, C_KB, C_VB, C_GB = 1280, 2048, 2816, 3584
C_UC, C_GC, C_MA, C_MB, C_MC = 3840, 4096, 4352, 5376, 6400
B_PAIRS = ((128, 1), (512, 4), (2048, 16))


def _slopes(n):
    return 2.0 ** (-8.0 * (np.arange(n, dtype=np.float64) + 1.0) / n)


def _tables():
    sa = _slopes(8)
    sb = _slopes(12)
    k = np.arange(128)[:, None].astype(np.float64)
    q = np.arange(128)[None, :].astype(np.float64)
    out = []
    for ch in range(10):
        if ch < 4:
            cs = [sa[2 * ch], sa[2 * ch + 1]]
            dil = 1
        else:
            g, jc = (ch - 4) // 2, (ch - 4) % 2
            dil = B_PAIRS[g][1]
            cs = [sb[g * 4 + 2 * jc] * dil, sb[g * 4 + 2 * jc + 1] * dil]
        tp = np.zeros((128, 2, 2, 128))
        ts = np.zeros((128, 4, 2))
        tn = np.zeros((4, 4, 2))
        for h in range(2):
            c = cs[h]
            dd = q - k
            tp[:, h, 0, :] = np.where(dd >= 0, np.exp(-c * np.maximum(dd, 0)), 0.0)
            do = 128 + q - k
            tp[:, h, 1, :] = np.where(do <= 128, np.exp(-c * np.minimum(do, 128.0)), 0.0)
            for j in range(4):
                m = np.arange(128)
                if dil == 1:
                    st = 128 + j - m
                    ts[:, j, h] = np.where(m >= j, np.exp(-c * st), 0.0)
                    for i in range(4):
                        tn[i, j, h] = np.exp(-c * (j - i)) if i <= j else 0.0
                else:
                    ts[:, j, h] = np.exp(-c * (128 - m))
                    tn[j, j, h] = 1.0
        out.append((tp.reshape(128, 512), ts.reshape(128, 8), tn))
    return out


class Prog:
    def __init__(self, nc, es):
        self.nc = nc
        self.es = es
        self.eng = {"pe": nc.tensor, "act": nc.scalar, "dve": nc.vector, "pool": nc.gpsimd, "sp": nc.sync}
        self.sem = {e: es.enter_context(nc.semaphore("c_" + e)) for e in self.eng}
        self.cnt = {e: 0 for e in self.eng}
        self.waited = {e: {} for e in self.eng}
        self.semobj = {}
        self.ndsem = 0

    def dsem(self, name):
        s = self.es.enter_context(self.nc.semaphore("d_" + name))
        self.ndsem += 1
        return [s, 0]

    def _wait(self, e, toks):
        best = {}
        for (s, v) in toks:
            k = id(s)
            self.semobj[k] = s
            if v > best.get(k, 0):
                best[k] = v
        for k, v in best.items():
            if self.waited[e].get(k, 0) < v:
                self.eng[e].wait_ge(self.semobj[k], v)
                self.waited[e][k] = v

    def _deps(self, e, reads, writes):
        toks = []
        mysem = self.sem[e]
        skip = (e == "pe")
        for r in reads:
            if r.w is not None:
                if r.w[0] is mysem and skip:
                    continue
                toks.append(r.w)
        for w in writes:
            if w.w is not None and not (w.w[0] is mysem and skip):
                toks.append(w.w)
            for t in w.r.values():
                if not (t[0] is mysem and skip):
                    toks.append(t)
        return toks

    def _mark(self, tok, reads, writes):
        for w in writes:
            w.w = tok
            w.r = {}
        for r in reads:
            k = id(tok[0])
            if k not in r.r or r.r[k][1] < tok[1]:
                r.r[k] = tok

    def op(self, e, fn, reads=(), writes=()):
        self._wait(e, self._deps(e, reads, writes))
        ins = fn(self.eng[e])
        self.cnt[e] += 1
        ins.then_inc(self.sem[e], 1)
        self._mark((self.sem[e], self.cnt[e]), reads, writes)

    def group(self, fns, reads=(), writes=()):
        e = "pe"
        self._wait(e, self._deps(e, reads, writes))
        ins = None
        for fn in fns:
            ins = fn(self.eng[e])
        self.cnt[e] += 1
        ins.then_inc(self.sem[e], 1)
        self._mark((self.sem[e], self.cnt[e]), reads, writes)

    def dma(self, q, ds, out, in_, reads=(), writes=()):
        toks = self._deps(q, reads, writes)
        if ds is None:
            ring = self.ring[q]
            ds = ring[self.ridx[q] % len(ring)]
            self.ridx[q] += 1
            if ds[1] > 0:
                toks.append((ds[0], ds[1]))
        self._wait(q, toks)
        ins = self.eng[q].dma_start(out=out, in_=in_)
        ds[1] += 16
        ins.then_inc(ds[0], 16)
        self._mark((ds[0], ds[1]), reads, writes)

    def mkrings(self, nsp, npool):
        self.ring = {"sp": [self.dsem("rs%d" % i) for i in range(nsp)], "pool": [self.dsem("rp%d" % i) for i in range(npool)]}
        self.ridx = {"sp": 0, "pool": 0}

    def all_dma_toks(self):
        return [(d[0], d[1]) for q in self.ring for d in self.ring[q] if d[1] > 0]

    def wait_all(self, e, toks):
        self._wait(e, toks)

    def barrier(self, dsems):
        toks = [(self.sem[f], self.cnt[f]) for f in self.eng if self.cnt[f] > 0]
        toks += [(d[0], d[1]) for d in dsems if d[1] > 0]
        toks += self.all_dma_toks()
        for e in self.eng:
            self._wait(e, toks)


def mkap(ref, dims):
    return bass.AP(ref.tensor, ref.offset, [list(ref.ap[0])] + [list(d) for d in dims])


class StopBuild(Exception):
    pass


STOP = None


def ckpt(name):
    if STOP == name:
        raise StopBuild()


class Res:
    __slots__ = ("w", "r", "n")

    def __init__(self, n=""):
        self.w = None
        self.r = {}
        self.n = n


def build(DEPTH, NS):
    TS = NS * 4
    TT = T + TS
    nc = bass.Bass("TRN2", target_bir_lowering=False)

    def din(name, shape):
        return nc.dram_tensor(name, list(shape), F32, kind="ExternalInput").ap()

    def dout(name, shape):
        return nc.dram_tensor(name, list(shape), F32, kind="ExternalOutput").ap()

    xp = din("xp", (T, D))
    xs = din("xs", (TS, D))
    sa = din("sa", (DEPTH, NS, 2, 128, 128))
    sb = [din("sb1", (DEPTH, NS, 2, 128, 256)), din("sb2", (DEPTH, NS, 2, 512, 256)),
          din("sb3", (DEPTH, NS, 2, 2048, 256))]
    sc = din("sc", (DEPTH, NS, 15, 256))
    nw = din("nw", (DEPTH, 128, 8))
    fnw = din("fnw", (128, D))
    w_in = din("w_in", (DEPTH, D, DIN))
    sinkx = din("sinkx", (DEPTH, 128, 4))
    wpa = din("wpa", (DEPTH, 512, D))
    wpb = din("wpb", (DEPTH, 256, D))
    wpc = din("wpc", (DEPTH, 256, D))
    wpool = din("wpool", (DEPTH, 4, 64, 64))
    pscale = din("pscale", (DEPTH, 128, 2))
    w_out = din("w_out", (DEPTH, D, D))
    tabs = din("tabs", (10, 128, 648))
    consts = din("consts", (128, 512))

    yp = dout("yp", (T, D))
    ys = dout("ys", (TS, D))
    nap = dout("nap", (DEPTH, 2, 128, 128))
    nas = dout("nas", (DEPTH, NS, 2, 128, 128))
    nbp = [dout("nb1p", (DEPTH, 2, 128, 256)), dout("nb2p", (DEPTH, 2, 512, 256)), dout("nb3p", (DEPTH, 2, 2048, 256))]
    nbs = [dout("nb1s", (DEPTH, NS, 2, 128, 256)), dout("nb2s", (DEPTH, NS, 2, 512, 256)),
           dout("nb3s", (DEPTH, NS, 2, 2048, 256))]
    ncp = dout("ncp", (DEPTH, 15, 256))
    ncs = dout("ncs", (DEPTH, NS, 15, 256))

    es = ExitStack()
    with es:
        def sb_t(name, shape, dt):
            return es.enter_context(nc.sbuf_tensor(name, list(shape), dt))

        P = Prog(nc, es)
        x = sb_t("x", (128, 17, D), F32)
        hT = sb_t("hT", (128, 8, TT), BF16)
        br = sb_t("br", (128, 8, TT), BF16)
        NSB = 2 if NS >= 2 else 1
        ATT_BYTES = 2 * ((TT * 2 + 3) // 4 * 4) + 17 * 2 * 128 * 2 + 2 * TT * 4 + 2048 + 3072 + 3 * 2048 + (NSB * 4 * 128 * 2) * 3 + 64 * 4 * 2
        SCR = max(ATT_BYTES, 8 * TT * 2, 3 * 2400 * 4 + TT * 2)
        scr = sb_t("scr", (128, SCR // 4), F32)
        hb = sb_t("hb", (128, D), BF16)
        wring = [sb_t("wr%d" % i, (128, 8, 256), BF16) for i in range(3)]
        stage = sb_t("stage", (128, 2, 256), F32)
        tab = sb_t("tab", (128, 648), F32)
        cst = sb_t("cst", (128, 160), F32)
        cbf = sb_t("cbf", (128, 384), BF16)
        small = sb_t("small", (128, 64), F32)
        nwt = sb_t("nwt", (128, 8), F32)
        sinkt = sb_t("sinkt", (128, 4), F32)
        pst = sb_t("pst", (128, 2), F32)
        wpl = sb_t("wpl", (128, 2, 128), BF16)
        qblk = sb_t("qblk", (128, 64, 2), BF16)
        VS = sb_t("VS", (128, 128), BF16)
        onesAllT = sb_t("onesAllT", (128, 128), BF16)
        psall = es.enter_context(nc.psum_tensor("psall", [128, 4096], F32))
        banks = [psall[:, i * 512:(i + 1) * 512] for i in range(8)]

        off = [0]

        def view(nbytes, dt, shape):
            esz = 4 if dt == F32 else 2
            o = off[0]
            off[0] += (nbytes + 3) // 4 * 4
            v = scr[:, o // 4:(o + nbytes + 3) // 4]
            if dt != F32:
                v = v.bitcast(dt)
            v = v[:, 0:nbytes // esz]
            if len(shape) == 3:
                v = v.rearrange("p (a b) -> p a b", a=shape[1])
            elif len(shape) == 4:
                v = v.rearrange("p (a b c) -> p a b c", a=shape[1], b=shape[2])
            return v
        QT = view(TT * 2, BF16, (128, TT))
        KT = view(TT * 2, BF16, (128, TT))
        Vpad = view(17 * 2 * 128 * 2, BF16, (128, 17, 2, 128))
        UZ = view(2 * TT * 4, F32, (128, 2, TT))
        Ebuf = view(2048, F32, (128, 512))
        PT = view(3072, BF16, (128, 3, 512))
        off_tA = off[0]
        tA = view(2048, F32, (128, 512))
        tB = view(2048, F32, (128, 512))
        tC = view(2048, F32, (128, 512))
        NTL = NSB * 4
        Kt = view(NTL * 128 * 2, BF16, (128, NTL, 128))
        Vt = view(NTL * 128 * 2, BF16, (128, NTL, 128))
        KTs = view(NTL * 128 * 2, BF16, (128, NTL, 128))
        Pbuf = view(64 * 4 * 2, BF16, (128, 2, 128))
        assert off[0] <= SCR, (off[0], SCR)
        mT = scr[:, 0:8 * TT // 2].bitcast(BF16).rearrange("p (a b) -> p a b", a=8)
        LB = 2400
        Ub = scr[:, 0:LB]
        Sa = scr[:, LB:2 * LB]
        Sb_ = scr[:, 2 * LB:3 * LB]
        diffT = scr[:, 3 * LB:3 * LB + TT // 2].bitcast(BF16)
        yb = scr[:, off_tA // 4:off_tA // 4 + 1024]
        fnwb = scr[:, 0:1024]

        R = {}

        def res(n):
            if n not in R:
                R[n] = Res(n)
            return R[n]
        Rb = [res("bank%d" % i) for i in range(8)]
        Rx = [res("x%d" % i) for i in range(17)]
        Rw = [res("w%d" % i) for i in range(3)]
        P.mkrings(24, 16)
        d_cp = P.dsem("cp")
        wctr = [0]

        ident = cbf[:, 0:128]
        onesE = cbf[:, 128:256]
        onesO = cbf[:, 256:384]

        def load_w(l, src, r0, nrows, c0, ncols, dcol=0, slot=None, kch=None):
            if slot is None:
                slot = wctr[0] % 3
                wctr[0] += 1
            kch = nrows // 128
            s = src[l, r0:r0 + nrows, c0:c0 + ncols].rearrange("(k p) c -> p k c", p=128)
            P.dma("pool", None, wring[slot][:, 0:kch, dcol:dcol + ncols], s, writes=[Rw[slot]])
            return slot

        def evac(e, out, in_, reads, writes, scale=None, func=None):
            if e == "act":
                kw = {}
                if scale is not None:
                    kw["scale"] = scale
                P.op("act", lambda a: a.activation(out=out, in_=in_, func=func or AF.Copy, **kw), reads, writes)
            else:
                if scale is not None:
                    P.op("dve", lambda v: v.tensor_scalar(out=out, in0=in_, scalar1=scale, scalar2=None, op0=ALU.mult), reads, writes)
                else:
                    P.op("dve", lambda v: v.tensor_copy(out=out, in_=in_), reads, writes)

        TCS = [(0, 512), (512, 512), (1024, 512), (1536, 512), (T, TS)]
        pbank = [0]

        def next_pbank():
            b = 6 + (pbank[0] % 2)
            pbank[0] += 1
            return b

        def fm_proj(slot, c0, kch, rhs_fn, t0, ln, b, extra_reads=()):
            fns = []
            for k in range(kch):
                fns.append(lambda t, k=k: t.matmul(banks[b][:, 0:ln], lhsT=wring[slot][:, k, c0:c0 + 128],
                                                   rhs=rhs_fn(k, t0, ln), start=(k == 0), stop=(k == kch - 1)))
            P.group(fns, reads=[Rw[slot]] + list(extra_reads), writes=[Rb[b]])

        def hT_rhs(k, t0, ln):
            return hT[:, k, t0:t0 + ln]

        Rh = res("hT")

        P.dma("sp", None, cst[:, 0:128], consts[:, 0:128], writes=[res("cst")])
        P.dma("sp", None, cst[:, 128:160], consts[:, 384:416], writes=[res("cst")])
        P.dma("pool", None, cbf[:], consts[:, 0:384], writes=[res("cbf")])
        for tt in range(16):
            P.dma("sp", None, x[:, tt, :], xp[tt * 128:(tt + 1) * 128, :], writes=[Rx[tt]])
        P.dma("sp", None, x[0:TS, 16, :], xs[:, :], writes=[Rx[16]])
        P.op("pool", lambda g: g.memset(qblk[:], 0.0), writes=[res("qblk")])
        pending = []

        def cp(dst, src):
            pending.append(lambda: P.dma("sp", d_cp, dst, src))

        for l in range(DEPTH):
            for (src, dst, W) in [(sa, nas, 128), (sb[0], nbs[0], 128)]:
                cp(dst[l, :, :, 0:W - 4, :].rearrange("n a r c -> (n a) (r c)"), src[l, :, :, 4:W, :].rearrange("n a r c -> (n a) (r c)"))
            cp(ncs[l, :, 0:11, :].rearrange("n r c -> n (r c)"), sc[l, :, 4:15, :].rearrange("n r c -> n (r c)"))
            for n in range(0, NS, 4):
                n1 = min(NS, n + 4)
                cp(nbs[1][l, n:n1, :, 0:508, :].rearrange("n a r c -> (n a) (r c)"), sb[1][l, n:n1, :, 4:512, :].rearrange("n a r c -> (n a) (r c)"))
            for n in range(NS):
                cp(nbs[2][l, n, :, 0:2044, :].rearrange("a r c -> a (r c)"), sb[2][l, n, :, 4:2048, :].rearrange("a r c -> a (r c)"))

        def issue_copies(k):
            for _ in range(k):
                if pending:
                    pending.pop(0)()
        ssq = small[:, 0:17]
        rstd = small[:, 17:34]
        mhalf = small[:, 34:51]
        P.op("dve", lambda v: v.memset(mhalf, -0.5), writes=[res("mhalf")])
        P.op("dve", lambda v: v.memset(Vpad[:, :, :, :], 0.0), writes=[res("Vpad")])
        P.op("dve", lambda v: v.memset(ssq, 1.0), writes=[res("ssq")])

        def rmsnorm_stats():
            for tt in range(17):
                rows = 128 if tt < 16 else TS
                P.op("act", lambda a: a.activation(out=hb[0:rows, :], in_=x[0:rows, tt, :], func=AF.Square, accum_out=ssq[0:rows, tt:tt + 1]),
                     reads=[Rx[tt]], writes=[res("hb"), res("ssq")])
            P.op("dve", lambda v: v.tensor_scalar(out=rstd, in0=ssq, scalar1=1.0 / D, scalar2=EPS, op0=ALU.mult, op1=ALU.add),
                 reads=[res("ssq")], writes=[res("rstd")])
            P.op("pool", lambda g: g.tensor_tensor(out=rstd, in0=rstd, in1=mhalf, op=ALU.pow),
                 reads=[res("rstd"), res("mhalf")], writes=[res("rstd")])
        ALLD = []
        P.op("pool", lambda g: g.memset(onesAllT[:], 1.0), writes=[res("onesAll")])
        stg = [0]

        def tm_mm(slot, c0, ncols, tok, rows):
            b = next_pbank()
            fns = []
            for k in range(8):
                fns.append(lambda t, k=k: t.matmul(banks[b][0:rows, 0:ncols], lhsT=tok(k),
                                                   rhs=wring[slot][:, k, c0:c0 + ncols], start=(k == 0), stop=(k == 7)))
            P.group(fns, reads=[Rw[slot], Rh], writes=[Rb[b]])
            return b

        def stage_out(b, rows, ncols, dsts, r0=0):
            s = stg[0] % 2
            stg[0] += 1
            rs = res("stage%d" % s)
            evac("dve", stage[0:rows, s, 0:ncols], banks[b][0:rows, 0:ncols], [Rb[b]], [rs])
            for dd in dsts:
                a0, a1, dst = dd[0:3]
                c0, c1 = (dd[3], dd[4]) if len(dd) > 3 else (0, ncols)
                P.dma("sp", None, dst, stage[a0:a1, s, c0:c1], reads=[rs])

        def gate(l, sg, c0, t0, ln):
            b = next_pbank()
            fm_proj(sg, c0, 8, hT_rhs, t0, ln, b, [Rh])
            P.op("act", lambda a: a.activation(out=tB[:, 0:ln], in_=banks[b][:, 0:ln], func=AF.Tanh, scale=0.5), [Rb[b]], [res("tB")])
            P.op("dve", lambda v: v.scalar_tensor_tensor(out=tA[:, 0:ln], in0=tB[:, 0:ln], scalar=1.0, in1=banks[b][:, 0:ln],
                                                         op0=ALU.add, op1=ALU.mult), [res("tB"), Rb[b]], [res("tA")])

        def uz_dst(which, it, d):
            c0, r, i0 = it
            if d == 1:
                return UZ[:, which, i0:i0 + 128]
            base = UZ[:, which, r + d * i0:r + d * i0 + 1]
            return mkap(base, [[d, 128]])

        def finalize(l, cur, d, mode, sg, br_k):
            bu, bz, lst = cur
            if mode in ("B0", "B1"):
                for it in lst:
                    c0 = it[0]
                    for which, bk in ((0, bu), (1, bz)):
                        dst = uz_dst(which, it, d)
                        if mode == "B0":
                            evac("act" if which == 0 else "dve", dst, banks[bk][:, c0:c0 + 128], [Rb[bk]], [res("UZ")])
                        else:
                            P.op("dve", lambda v, dst=dst, bk=bk, c0=c0: v.tensor_tensor(out=dst, in0=dst, in1=banks[bk][:, c0:c0 + 128], op=ALU.add),
                                 [Rb[bk], res("UZ")], [res("UZ")])
                return
            t0 = lst[0][2]
            gate(l, sg, 0, t0, 512)
            if mode == "A":
                P.op("dve", lambda v: v.tensor_scalar(out=tB[:, :], in0=banks[bz][:, :], scalar1=sinkt[:, br_k:br_k + 1], scalar2=None, op0=ALU.add),
                     [Rb[bz], res("sinkt"), res("tB")], [res("tB")])
                P.op("dve", lambda v: v.reciprocal(out=tB[:, :], in_=tB[:, :]), [res("tB")], [res("tB")])
                P.op("dve", lambda v: v.scalar_tensor_tensor(out=tC[:, :], in0=banks[bu][:, :], scalar=0.5, in1=tB[:, :], op0=ALU.mult, op1=ALU.mult),
                     [Rb[bu], res("tB")], [res("tC")])
            else:
                P.op("dve", lambda v: v.tensor_tensor(out=tB[:, :], in0=UZ[:, 1, t0:t0 + 512], in1=banks[bz][:, :], op=ALU.add),
                     [Rb[bz], res("UZ"), res("tB")], [res("tB")])
                P.op("dve", lambda v: v.reciprocal(out=tB[:, :], in_=tB[:, :]), [res("tB")], [res("tB")])
                P.op("dve", lambda v: v.tensor_tensor(out=tC[:, :], in0=UZ[:, 0, t0:t0 + 512], in1=banks[bu][:, :], op=ALU.add),
                     [Rb[bu], res("UZ")], [res("tC")])
                P.op("dve", lambda v: v.scalar_tensor_tensor(out=tC[:, :], in0=tC[:, :], scalar=0.5, in1=tB[:, :], op0=ALU.mult, op1=ALU.mult),
                     [res("tC"), res("tB")], [res("tC")])
            P.op("dve", lambda v: v.tensor_tensor(out=br[:, br_k, t0:t0 + 512], in0=tC[:, :], in1=tA[:, :], op=ALU.mult),
                 [res("tC"), res("tA")], [res("br%d" % br_k)])

        def sample_attention(l, d, state, st_c0, st_dup, mode, sg, br_k):
            qs = QT[:, T:T + TS]
            P.op("dve", lambda v: v.tensor_copy(out=qblk[0:64, 0:TS, 0], in_=qs[0:64, :]), [res("QT")], [res("qblk")])
            P.op("dve", lambda v: v.tensor_copy(out=qblk[64:128, 0:TS, 1], in_=qs[64:128, :]), [res("QT")], [res("qblk")])
            ncol = TS * 2
            bS, bN, bU, bZ = 0, 1, 2, 3
            qb2 = qblk[:, 0:TS, :].rearrange("p u h -> p (u h)")
            P.group([lambda t: t.matmul(banks[bN][0:TS, 0:ncol], lhsT=KT[:, T:T + TS], rhs=qb2, start=True, stop=True)],
                    reads=[res("KT"), res("qblk")], writes=[Rb[bN]])
            pn = Pbuf[0:TS, 1, 0:ncol]
            P.op("act", lambda a: a.activation(out=tB[0:TS, 0:ncol], in_=banks[bN][0:TS, 0:ncol], func=AF.Exp), [Rb[bN]], [res("tB")])
            P.op("dve", lambda v: v.tensor_tensor(out=pn, in0=tB[0:TS, 0:ncol], in1=tab[0:TS, 520:520 + ncol], op=ALU.mult),
                 [res("tB"), res("tab")], [res("Pn")])
            npn = 1 if d == 1 else 4
            for n0 in range(0, NS, NSB):
                nn = min(NSB, NS - n0)
                rowst = state.shape[-1]
                wcols = 64 if st_dup else 128
                for kv, dstt, rn in ((0, Kt, "Kt"), (1, Vt, "Vt")):
                    if npn == 1:
                        b0 = state[l, n0, kv, 0:1, st_c0:st_c0 + 1]
                        src = bass.AP(b0.tensor, b0.offset, [[rowst, 128], [2 * state.shape[3] * rowst, nn], [1, wcols]])
                        for half in range(2 if st_dup else 1):
                            dd = mkap(dstt[:, 0:1, half * 64:half * 64 + 1], [[128, nn], [1, wcols]])
                            P.dma("pool", None, dd, src, writes=[res(rn)])
                    else:
                        for n in range(n0, n0 + nn):
                            b0 = state[l, n, kv, 0:1, st_c0:st_c0 + 1]
                            src = bass.AP(b0.tensor, b0.offset, [[d * rowst, 128], [rowst, npn], [1, wcols]])
                            dd = mkap(dstt[:, (n - n0) * npn:(n - n0) * npn + 1, 0:1], [[128, npn], [1, wcols]])
                            P.dma("pool", None, dd, src, writes=[res(rn)])
                ntl = nn * npn
                b = next_pbank()
                pb16 = banks[b][:, :].bitcast(BF16)
                fns = []
                for ti in range(ntl):
                    fns.append(lambda t, ti=ti: t.transpose(out=pb16[:, ti * 128:(ti + 1) * 128], in_=Kt[:, ti, :], identity=ident))
                P.group(fns, reads=[res("Kt"), res("cbf")], writes=[Rb[b]])
                evac("dve", KTs[:, 0:ntl, :].rearrange("p a b -> p (a b)"), pb16[:, 0:ntl * 128], [Rb[b]], [res("KTs")])
                cols = []
                for n in range(n0, n0 + nn):
                    for jt in range(npn):
                        ti = (n - n0) * npn + jt
                        cols.append((ti, n * 8, 8) if d == 1 else (ti, (n * 4 + jt) * 2, 2))
                fns = [lambda t, ti=ti, cc0=cc0, cn=cn: t.matmul(banks[bS][:, cc0:cc0 + cn], lhsT=KTs[:, ti, :], rhs=qb2[:, cc0:cc0 + cn],
                                                                 start=True, stop=True, skip_group_check=True) for (ti, cc0, cn) in cols]
                P.group(fns, reads=[res("KTs"), res("qblk")], writes=[Rb[bS]])
                c_lo, c_hi = n0 * 8, (n0 + nn) * 8
                P.op("act", lambda a, c_lo=c_lo, c_hi=c_hi: a.activation(out=tC[:, c_lo:c_hi], in_=banks[bS][:, c_lo:c_hi], func=AF.Exp), [Rb[bS]], [res("tC")])
                tsb = tab[:, 512:520]
                tsb3 = mkap(tsb, [[0, nn], [1, 8]])
                P.op("dve", lambda v, c_lo=c_lo, c_hi=c_hi, tsb3=tsb3, nn=nn: v.tensor_tensor(
                    out=Pbuf[:, 0, c_lo:c_hi].rearrange("p (n c) -> p n c", n=nn), in0=tC[:, c_lo:c_hi].rearrange("p (n c) -> p n c", n=nn),
                    in1=tsb3, op=ALU.mult), [res("tC"), res("tab")], [res("Pb")])
                fu, fz = [], []
                if n0 == 0:
                    fu.append(lambda t: t.matmul(banks[bU][:, 0:ncol], lhsT=VS[0:TS, :], rhs=pn, start=True, stop=False, skip_group_check=True))
                    fz.append(lambda t: t.matmul(banks[bZ][:, 0:ncol], lhsT=onesAllT[0:TS, :], rhs=pn, start=True, stop=False, skip_group_check=True))
                for (ti, cc0, cn) in cols:
                    fu.append(lambda t, ti=ti, cc0=cc0, cn=cn: t.matmul(banks[bU][:, cc0:cc0 + cn], lhsT=Vt[:, ti, :], rhs=Pbuf[:, 0, cc0:cc0 + cn],
                                                                          start=False, stop=True, skip_group_check=True))
                fz.append(lambda t, c_lo=c_lo, c_hi=c_hi: t.matmul(banks[bZ][:, c_lo:c_hi], lhsT=onesAllT[:, :], rhs=Pbuf[:, 0, c_lo:c_hi],
                                                                     start=False, stop=True, skip_group_check=True))
                P.group(fu, reads=[res("Vt"), res("Pb"), res("Pn"), res("VS")], writes=[Rb[bU]])
                P.group(fz, reads=[res("Pb"), res("Pn"), res("onesAll")], writes=[Rb[bZ]])
            u3 = banks[bU][:, 0:ncol].rearrange("p (u h) -> p u h", h=2)
            z3 = banks[bZ][:, 0:ncol].rearrange("p (u h) -> p u h", h=2)
            halves = ((0, 64, 0), (64, 128, 1))
            if mode in ("B0", "B1"):
                for (p0, p1, h) in halves:
                    for which, src in ((0, u3), (1, z3)):
                        dst = UZ[p0:p1, which, T:T + TS]
                        bk = bU if which == 0 else bZ
                        if mode == "B0":
                            evac("dve", dst, src[p0:p1, :, h], [Rb[bk]], [res("UZ")])
                        else:
                            P.op("dve", lambda v, dst=dst, src=src, p0=p0, p1=p1, h=h: v.tensor_tensor(out=dst, in0=dst, in1=src[p0:p1, :, h], op=ALU.add),
                                 [Rb[bk], res("UZ")], [res("UZ")])
                return
            gate(l, sg, 0, T, TS)
            for (p0, p1, h) in halves:
                if mode == "A":
                    P.op("dve", lambda v, p0=p0, p1=p1, h=h: v.tensor_scalar(out=tB[p0:p1, 0:TS], in0=z3[p0:p1, :, h], scalar1=sinkt[p0:p1, br_k:br_k + 1],
                                                                            scalar2=None, op0=ALU.add), [Rb[bZ], res("sinkt"), res("tB")], [res("tB")])
                    P.op("dve", lambda v, p0=p0, p1=p1: v.reciprocal(out=tB[p0:p1, 0:TS], in_=tB[p0:p1, 0:TS]), [res("tB")], [res("tB")])
                    P.op("dve", lambda v, p0=p0, p1=p1, h=h: v.scalar_tensor_tensor(out=tC[p0:p1, 0:TS], in0=u3[p0:p1, :, h], scalar=0.5, in1=tB[p0:p1, 0:TS],
                                                                                   op0=ALU.mult, op1=ALU.mult), [Rb[bU], res("tB"), res("tC")], [res("tC")])
                else:
                    P.op("dve", lambda v, p0=p0, p1=p1, h=h: v.tensor_tensor(out=tB[p0:p1, 0:TS], in0=UZ[p0:p1, 1, T:T + TS], in1=z3[p0:p1, :, h], op=ALU.add),
                         [Rb[bZ], res("UZ"), res("tB")], [res("tB")])
                    P.op("dve", lambda v, p0=p0, p1=p1: v.reciprocal(out=tB[p0:p1, 0:TS], in_=tB[p0:p1, 0:TS]), [res("tB")], [res("tB")])
                    P.op("dve", lambda v, p0=p0, p1=p1, h=h: v.tensor_tensor(out=tC[p0:p1, 0:TS], in0=UZ[p0:p1, 0, T:T + TS], in1=u3[p0:p1, :, h], op=ALU.add),
                         [Rb[bU], res("UZ"), res("tC")], [res("tC")])
                    P.op("dve", lambda v, p0=p0, p1=p1: v.scalar_tensor_tensor(out=tC[p0:p1, 0:TS], in0=tC[p0:p1, 0:TS], scalar=0.5, in1=tB[p0:p1, 0:TS],
                                                                             op0=ALU.mult, op1=ALU.mult), [res("tC"), res("tB")], [res("tC")])
            P.op("dve", lambda v: v.tensor_tensor(out=br[:, br_k, T:T + TS], in0=tC[:, 0:TS], in1=tA[:, 0:TS], op=ALU.mult),
                 [res("tC"), res("tA")], [res("br%d" % br_k)])

        def attention(l, ch, seqs, d, q_c0, k_c0, v_c0, dup, g_c0, state, st_c0, W, kout, kouts, out_c0, do_out, mode, br_k):
            P.dma("sp", None, tab[:], tabs[ch, :, :], writes=[res("tab")])
            L = T // d
            sq = load_w(l, w_in, 0, D, q_c0, 128)
            sk = wctr[0] % 3
            wctr[0] += 1
            if dup:
                for dc_, c_ in ((0, k_c0), (64, k_c0), (128, v_c0), (192, v_c0)):
                    load_w(l, w_in, 0, D, c_, 64, dc_, slot=sk)
            else:
                load_w(l, w_in, 0, D, k_c0, 128, 0, slot=sk)
                load_w(l, w_in, 0, D, v_c0, 128, 128, slot=sk)
            for (dst, rn, slot, scl, e) in [(QT, "QT", sq, 0.125, "act"), (KT, "KT", sk, None, "dve")]:
                for (t0, ln) in TCS:
                    b = next_pbank()
                    fm_proj(slot, 0, 8, hT_rhs, t0, ln, b, [Rh])
                    if t0 < T and d > 1:
                        src = banks[b][:, 0:ln].rearrange("p (a r) -> p r a", r=d)
                        o1 = dst[:, t0 // d:t0 // d + 1]
                        dd = mkap(o1, [[L, d], [1, ln // d]])
                    else:
                        src = banks[b][:, 0:ln]
                        dd = dst[:, t0:t0 + ln]
                    evac(e, dd, src, [Rb[b]], [res(rn)], scale=scl)
            ckpt('att1')
            ckpt('att1' + mode)
            tiles = []
            for (r, Ls) in seqs:
                for bb in range(Ls // 128):
                    tiles.append((r, bb))
            nv = 64 if dup else 128
            for ti in range(17):
                if ti < 16:
                    r, bb = tiles[ti]
                    start = r + d * 128 * bb
                    rows = 128
                    tok = (lambda k, start=start: hT[:, k, start:start + d * 127 + 1:d])
                else:
                    rows = TS
                    tok = (lambda k: hT[:, k, T:T + TS])
                b = tm_mm(sk, 128, 128, tok, rows)
                evac("dve", Vpad[0:rows, ti, 0, 0:64], banks[b][0:rows, 0:64], [Rb[b]], [res("Vpad")])
                evac("act", Vpad[0:rows, ti, 1, 64:128], banks[b][0:rows, (0 if dup else 64):(64 if dup else 128)], [Rb[b]], [res("Vpad")])
                if ti == 16:
                    evac("act", VS[0:TS, :], banks[b][0:TS, 0:128], [Rb[b]], [res("VS")])
                    if do_out:
                        stage_out(b, TS, nv, [(4 * n, 4 * n + 4, kouts[l, n, 1, W - 4:W, out_c0:out_c0 + nv]) for n in range(NS)])
                elif do_out and d == 1 and start >= T - W:
                    s0 = start - (T - W)
                    stage_out(b, 128, nv, [(0, 128, kout[l, 1, s0:s0 + 128, out_c0:out_c0 + nv])])
            ckpt('att2')
            ckpt('att2' + mode)
            if do_out:
                for tt in range(16):
                    if tt * 128 >= T - W:
                        s0 = tt * 128 - (T - W)
                        if d > 1:
                            b = tm_mm(sk, 0, 256, (lambda k, tt=tt: hT[:, k, tt * 128:(tt + 1) * 128]), 128)
                            stage_out(b, 128, 256, [(0, 128, kout[l, 0, s0:s0 + 128, out_c0:out_c0 + 128], 0, 128),
                                                    (0, 128, kout[l, 1, s0:s0 + 128, out_c0:out_c0 + 128], 128, 256)])
                        else:
                            b = tm_mm(sk, 0, nv, (lambda k, tt=tt: hT[:, k, tt * 128:(tt + 1) * 128]), 128)
                            stage_out(b, 128, nv, [(0, 128, kout[l, 0, s0:s0 + 128, out_c0:out_c0 + nv])])
                b = tm_mm(sk, 0, nv, (lambda k: hT[:, k, T:T + TS]), TS)
                stage_out(b, TS, nv, [(4 * n, 4 * n + 4, kouts[l, n, 0, W - 4:W, out_c0:out_c0 + nv]) for n in range(NS)])
            issue_copies(2)
            ckpt('att3')
            ckpt('att3' + mode)
            sg = load_w(l, w_in, 0, D, g_c0, 128)
            RQ, RK, RV = res("QT"), res("KT"), res("Vpad")
            uzb = 0
            win_cols = 0
            cur = None
            kbg = 0
            for (r, Ls) in seqs:
                nb = Ls // 128
                base = (r * L) if d > 1 else 0
                for kb in range(nb):
                    k0 = base + kb * 128
                    nq = 256 if kb < nb - 1 else 128
                    fns = [lambda t, h=h, k0=k0, nq=nq: t.matmul(
                        banks[h][:, 0:nq], lhsT=KT[h * 64:(h + 1) * 64, k0:k0 + 128],
                        rhs=QT[h * 64:(h + 1) * 64, k0:k0 + nq], start=True, stop=True, skip_group_check=True) for h in range(2)]
                    P.group(fns, reads=[RQ, RK], writes=[Rb[0], Rb[1]])
                    s3 = psall[:, 0:1024].rearrange("p (h q) -> p h q", h=2)[:, :, 0:nq]
                    e3 = Ebuf.rearrange("p (h q) -> p h q", h=2)[:, :, 0:nq]
                    t3 = tab[:, 0:512].rearrange("p (h q) -> p h q", h=2)[:, :, 0:nq]
                    pslot = kbg % 3
                    p3 = PT[:, pslot, :].rearrange("p (h q) -> p h q", h=2)[:, :, 0:nq]
                    P.op("act", lambda a, s3=s3, e3=e3: a.activation(out=e3, in_=s3, func=AF.Exp), [Rb[0], Rb[1]], [res("E")])
                    P.op("dve", lambda v, e3=e3, t3=t3, p3=p3: v.tensor_tensor(out=p3, in0=e3, in1=t3, op=ALU.mult),
                         [res("E"), res("tab")], [res("PT%d" % pslot)])
                    if STOP == 'lp1':
                        kbg += 1
                        continue
                    if win_cols == 0:
                        cur = (2 + 2 * (uzb % 2), 3 + 2 * (uzb % 2), [])
                        uzb += 1
                    bu, bz, lst = cur
                    c0 = win_cols
                    parts = ([(kbg - 1, 128)] if kb > 0 else []) + [(kbg, 0)]
                    fu, fz = [], []
                    npart = 2 * len(parts)
                    idx = 0
                    for h in range(2):
                        for (tk, qo) in parts:
                            st_, sp_ = (idx == 0 and c0 == 0), (idx == npart - 1)
                            fu.append(lambda t, h=h, tk=tk, qo=qo, st_=st_, sp_=sp_, c0=c0, bu=bu: t.matmul(
                                banks[bu][:, c0:c0 + 128], lhsT=Vpad[:, tk, h, :], rhs=PT[:, tk % 3, h * 256 + qo:h * 256 + qo + 128],
                                start=st_, stop=sp_, skip_group_check=True))
                            fz.append(lambda t, h=h, tk=tk, qo=qo, st_=st_, sp_=sp_, c0=c0, bz=bz: t.matmul(
                                banks[bz][:, c0:c0 + 128], lhsT=(onesE if h == 0 else onesO), rhs=PT[:, tk % 3, h * 256 + qo:h * 256 + qo + 128],
                                start=st_, stop=sp_, skip_group_check=True))
                            idx += 1
                    rp = [res("PT%d" % (tk % 3)) for (tk, _) in parts]
                    P.group(fu, reads=rp + [RV], writes=[Rb[bu]])
                    P.group(fz, reads=rp + [res("cbf")], writes=[Rb[bz]])
                    lst.append((c0, r, kb * 128))
                    win_cols += 128
                    kbg += 1
                    if win_cols == 512:
                        if STOP != 'lp2':
                            finalize(l, cur, d, mode, sg, br_k)
                        win_cols = 0
            ckpt('att4')
            ckpt('att4' + mode)
            ckpt('lp1')
            ckpt('lp2')
            sample_attention(l, d, state, st_c0, dup, mode, sg, br_k)
            ckpt('end' + mode)

        def pool_phase(l):
            issue_copies(3)
            P.op("pool", lambda g: g.memset(wpl[:], 0.0), writes=[res("wpl")])
            for gi in range(4):
                p0 = 64 * (gi % 2)
                P.dma("pool", None, wpl[p0:p0 + 64, gi // 2, p0:p0 + 64], wpool[l, gi, :, :], writes=[res("wpl")])
            P.op("dve", lambda v: v.tensor_scalar(out=pst[:], in0=pst[:], scalar1=0.5, scalar2=None, op0=ALU.mult), [res("pst")], [res("pst")])
            s2 = load_w(l, w_in, 0, D, C_UC, 256)
            b = tm_mm(s2, 0, 256, (lambda k: hT[:, k, T - 128:T]), 128)
            stage_out(b, 128, 256, [(113, 128, ncp[l, :, :])])
            b = tm_mm(s2, 0, 256, (lambda k: hT[:, k, T:T + TS]), TS)
            stage_out(b, TS, 256, [(4 * n, 4 * n + 4, ncs[l, n, 11:15, :]) for n in range(NS)])
            SB0 = 16 + T
            NL = SB0 + 19 * NS
            for cc in range(2):
                su = load_w(l, w_in, 0, D, C_UC + 128 * cc, 128)
                sgc = load_w(l, w_in, 0, D, C_GC + 128 * cc, 128)
                RU = res("Ub")
                P.op("dve", lambda v: v.memset(Ub[:, 0:16], 0.0), writes=[RU])
                for (t0, ln) in TCS:
                    b = next_pbank()
                    fm_proj(su, 0, 8, hT_rhs, t0, ln, b, [Rh])
                    if t0 < T:
                        evac("dve", Ub[:, 16 + t0:16 + t0 + ln], banks[b][:, 0:ln], [Rb[b]], [RU])
                    else:
                        o1 = Ub[:, SB0 + 15:SB0 + 16]
                        dd = mkap(o1, [[19, NS], [1, 4]])
                        evac("dve", dd, banks[b][:, 0:TS].rearrange("p (n j) -> p n j", j=4), [Rb[b]], [RU])
                for n0 in range(0, NS, 8):
                    nn = min(8, NS - n0)
                    rows = nn * 15
                    P.dma("sp", None, tC[0:rows, 0:256], sc[l, n0:n0 + nn, :, :].rearrange("n r c -> (n r) c"), writes=[res("tC")])
                    b = next_pbank()
                    P.group([lambda t: t.transpose(out=banks[b][:, 0:rows], in_=tC[0:rows, cc * 128:(cc + 1) * 128], identity=cst[0:rows, 0:rows])],
                            reads=[res("tC"), res("cst")], writes=[Rb[b]])
                    o1 = Ub[:, SB0 + 19 * n0:SB0 + 19 * n0 + 1]
                    dd = mkap(o1, [[19, nn], [1, 15]])
                    evac("dve", dd, banks[b][:, 0:rows].rearrange("p (n r) -> p n r", r=15), [Rb[b]], [RU])
                RSa, RSb = res("Sa"), res("Sb")
                P.op("dve", lambda v: v.tensor_tensor(out=Sa[:, 1:NL], in0=Ub[:, 1:NL], in1=Ub[:, 0:NL - 1], op=ALU.add), [RU], [RSa])
                P.op("dve", lambda v: v.tensor_tensor(out=Sb_[:, 3:NL], in0=Sa[:, 3:NL], in1=Sa[:, 1:NL - 2], op=ALU.add), [RSa], [RSb])
                if cc == 1:
                    P.op("dve", lambda v: v.tensor_tensor(out=Sa[:, 7:NL], in0=Sb_[:, 7:NL], in1=Sb_[:, 3:NL - 4], op=ALU.add), [RSb, RSa], [RSa])
                    P.op("dve", lambda v: v.tensor_tensor(out=Sb_[:, 15:NL], in0=Sa[:, 15:NL], in1=Sa[:, 7:NL - 8], op=ALU.add), [RSa, RSb], [RSb])
                ws = (2, 4) if cc == 0 else (8, 16)
                RD = res("diffT")
                for hi, (S_, RS) in enumerate(((Sa, RSa), (Sb_, RSb))):
                    p0, p1 = 64 * hi, 64 * hi + 64
                    iw = 1.0 / ws[hi]
                    P.op("dve", lambda v, S_=S_, p0=p0, p1=p1, iw=iw: v.scalar_tensor_tensor(out=diffT[p0:p1, 0:T], in0=S_[p0:p1, 16:16 + T], scalar=iw,
                                                                                           in1=Ub[p0:p1, 16:16 + T], op0=ALU.mult, op1=ALU.subtract), [RS, RU], [RD])
                    o1 = S_[p0:p1, SB0 + 15:SB0 + 16]
                    sv = mkap(o1, [[19, NS], [1, 4]])
                    o2 = Ub[p0:p1, SB0 + 15:SB0 + 16]
                    uv = mkap(o2, [[19, NS], [1, 4]])
                    P.op("dve", lambda v, sv=sv, uv=uv, p0=p0, p1=p1, iw=iw: v.scalar_tensor_tensor(
                        out=diffT[p0:p1, T:T + TS].rearrange("p (n j) -> p n j", j=4), in0=sv, scalar=iw, in1=uv, op0=ALU.mult, op1=ALU.subtract), [RS, RU], [RD])
                    ic = cst[p0:p1, 128 + 16 * cc:144 + 16 * cc]
                    P.op("dve", lambda v, S_=S_, p0=p0, p1=p1, ic=ic: v.tensor_tensor(out=tC[p0:p1, 0:16], in0=S_[p0:p1, 16:32], in1=ic, op=ALU.mult),
                         [RS, res("cst"), res("tC")], [res("tC")])
                    P.op("dve", lambda v, p0=p0, p1=p1: v.tensor_tensor(out=diffT[p0:p1, 0:16], in0=tC[p0:p1, 0:16], in1=Ub[p0:p1, 16:32], op=ALU.subtract),
                         [res("tC"), RU, RD], [RD])
                for (t0, ln) in TCS:
                    b = next_pbank()
                    P.group([lambda t: t.matmul(banks[b][:, 0:ln], lhsT=wpl[:, cc, :], rhs=diffT[:, t0:t0 + ln], start=True, stop=True)],
                            reads=[res("wpl"), RD], writes=[Rb[b]])
                    gate(l, sgc, 0, t0, ln)
                    P.op("dve", lambda v: v.scalar_tensor_tensor(out=br[:, 6 + cc, t0:t0 + ln], in0=banks[b][:, 0:ln], scalar=pst[:, cc:cc + 1], in1=tA[:, 0:ln],
                                                                 op0=ALU.mult, op1=ALU.mult), [Rb[b], res("pst"), res("tA")], [res("br%d" % (6 + cc))])

        def final_phase(l):
            Rm = res("mT")
            for dc in range(8):
                sX = wctr[0] % 3
                sY = (wctr[0] + 1) % 3
                sZ = (wctr[0] + 2) % 3
                wctr[0] += 3
                load_w(l, w_in, 0, D, C_MA + 128 * dc, 128, 0, slot=sX)
                load_w(l, w_in, 0, D, C_MB + 128 * dc, 128, 128, slot=sX)
                load_w(l, w_in, 0, D, C_MC + 128 * dc, 128, 0, slot=sY)
                load_w(l, wpc, 0, 256, 128 * dc, 128, 128, slot=sY)
                load_w(l, wpa, 0, 512, 128 * dc, 128, 0, slot=sZ)
                load_w(l, wpb, 0, 256, 128 * dc, 128, 128, slot=sZ)
                for (t0, ln) in TCS:
                    fm_proj(sZ, 0, 4, (lambda k, t0, ln: br[:, k, t0:t0 + ln]), t0, ln, 0, [res("br%d" % i) for i in range(4)])
                    fm_proj(sZ, 128, 2, (lambda k, t0, ln: br[:, 4 + k, t0:t0 + ln]), t0, ln, 1, [res("br4"), res("br5")])
                    fm_proj(sY, 128, 2, (lambda k, t0, ln: br[:, 6 + k, t0:t0 + ln]), t0, ln, 2, [res("br6"), res("br7")])
                    fm_proj(sX, 0, 8, hT_rhs, t0, ln, 3, [Rh])
                    fm_proj(sX, 128, 8, hT_rhs, t0, ln, 4, [Rh])
                    fm_proj(sY, 0, 8, hT_rhs, t0, ln, 5, [Rh])
                    for i, tt_ in enumerate((tA, tB, tC)):
                        P.op("act", lambda a, i=i, tt_=tt_: a.activation(out=tt_[:, 0:ln], in_=banks[3 + i][:, 0:ln], func=AF.Tanh, scale=0.5),
                             [Rb[3 + i]], [res("t" + "ABC"[i])])
                        P.op("dve", lambda v, i=i, tt_=tt_: v.scalar_tensor_tensor(out=tt_[:, 0:ln], in0=tt_[:, 0:ln], scalar=1.0, in1=banks[i][:, 0:ln],
                                                                                 op0=ALU.add, op1=ALU.mult), [res("t" + "ABC"[i]), Rb[i]], [res("t" + "ABC"[i])])
                    P.op("dve", lambda v: v.tensor_tensor(out=tA[:, 0:ln], in0=tA[:, 0:ln], in1=tB[:, 0:ln], op=ALU.add), [res("tA"), res("tB")], [res("tA")])
                    P.op("dve", lambda v: v.tensor_tensor(out=mT[:, dc, t0:t0 + ln], in0=tA[:, 0:ln], in1=tC[:, 0:ln], op=ALU.add), [res("tA"), res("tC")], [Rm])
            for q in range(4):
                so = load_w(l, w_out, 0, D, 256 * q, 256)
                for tt in range(17):
                    rows = 128 if tt < 16 else TS
                    t0 = tt * 128
                    b = next_pbank()
                    fns = [lambda t, k=k: t.matmul(banks[b][0:rows, 0:256], lhsT=mT[:, k, t0:t0 + rows], rhs=wring[so][:, k, 0:256],
                                                   start=(k == 0), stop=(k == 7)) for k in range(8)]
                    P.group(fns, reads=[Rw[so], Rm], writes=[Rb[b]])
                    P.op("dve", lambda v: v.scalar_tensor_tensor(out=x[0:rows, tt, 256 * q:256 * q + 256], in0=banks[b][0:rows, 0:256], scalar=0.5,
                                                                 in1=x[0:rows, tt, 256 * q:256 * q + 256], op0=ALU.mult, op1=ALU.add), [Rb[b], Rx[tt]], [Rx[tt]])

        try:
          for l in range(DEPTH):
              ckpt('start')
              if l > 0:
                  P.op("pool", lambda g: g.memset(Vpad[:, :, 0, 64:128], 0.0), writes=[res("Vpad")])
                  P.op("pool", lambda g: g.memset(Vpad[:, :, 1, 0:64], 0.0), writes=[res("Vpad")])
              P.dma("sp", None, nwt[:], nw[l, :, :], writes=[res("nwt")])
              P.dma("sp", None, sinkt[:], sinkx[l, :, :], writes=[res("sinkt")])
              P.dma("sp", None, pst[:], pscale[l, :, :], writes=[res("pst")])
              P.op("act", lambda a: a.activation(out=sinkt[:], in_=sinkt[:], func=AF.Exp), reads=[res("sinkt")], writes=[res("sinkt")])
              rmsnorm_stats()
              for tt in range(17):
                  rows = 128 if tt < 16 else TS
                  t0 = tt * 128
                  P.op("act", lambda a: a.activation(out=hb[0:rows, :], in_=x[0:rows, tt, :], func=AF.Copy, scale=rstd[0:rows, tt:tt + 1]),
                       reads=[Rx[tt], res("rstd")], writes=[res("hb")])
                  b = next_pbank()
                  pb16 = banks[b][:, :].bitcast(BF16)
                  fns = [lambda t, k=k: t.transpose(out=pb16[:, k * 128:k * 128 + rows], in_=hb[0:rows, k * 128:(k + 1) * 128],
                                                    identity=ident[0:rows, 0:rows]) for k in range(8)]
                  P.group(fns, reads=[res("hb"), res("cbf")], writes=[Rb[b]])
                  for k in range(8):
                      evac("act" if k % 2 == 0 else "dve", hT[:, k, t0:t0 + rows], pb16[:, k * 128:k * 128 + rows], [Rb[b], res("nwt")], [Rh],
                           scale=nwt[:, k:k + 1])
              ckpt('R')
              for c in range(4):
                  kvh = c // 2
                  attention(l, c, [(0, T)], 1, C_QA + 128 * c, C_KA + 64 * kvh, C_VA + 64 * kvh, True, C_GA + 128 * c, sa, 64 * kvh, 128,
                            nap, nas, 64 * kvh, (c % 2 == 0), "A", c)
                  ckpt('A%d' % c)
              for jc in range(2):
                  for gi, g in enumerate((2, 1, 0)):
                      Wg, dg = B_PAIRS[g]
                      hc = g * 256 + jc * 128
                      attention(l, 4 + g * 2 + jc, [(r, T // dg) for r in range(dg)], dg, C_QB + hc, C_KB + hc, C_VB + hc, False, C_GB + 128 * jc,
                                sb[g], jc * 128, Wg, nbp[g], nbs[g], jc * 128, True, ("B0", "B1", "B2")[gi], 4 + jc)
              ckpt('B')
              P.barrier(ALLD)
              pool_phase(l)
              ckpt('C')
              P.barrier(ALLD)
              final_phase(l)
              ckpt('F')
              P.barrier(ALLD)
        except StopBuild:
            pass
        issue_copies(len(pending))
        P.dma("sp", None, fnwb, fnw[:, :], writes=[res("fnwb")])
        rmsnorm_stats()
        for tt in range(17):
            rows = 128 if tt < 16 else TS
            P.op("dve", lambda v: v.scalar_tensor_tensor(out=yb[0:rows, :], in0=x[0:rows, tt, :], scalar=rstd[0:rows, tt:tt + 1], in1=fnwb[0:rows, :],
                                                         op0=ALU.mult, op1=ALU.mult), [Rx[tt], res("rstd"), res("fnwb"), res("yb")], [res("yb")])
            dst = yp[tt * 128:(tt + 1) * 128, :] if tt < 16 else ys[:, :]
            P.dma("sp", None, dst, yb[0:rows, :], reads=[res("yb")], writes=[res("yb_dma")])
        P.wait_all("sp", P.all_dma_toks() + [(d_cp[0], d_cp[1])])
    return nc


def _host_consts(NS):
    tb = _tables()
    TS = NS * 4
    tabs = np.zeros((10, 128, 648), np.float32)
    for ch, (tp, ts, tn) in enumerate(tb):
        tabs[ch, :, 0:512] = tp
        tabs[ch, :, 512:520] = ts
        full = np.zeros((64, 16, 4, 2))
        for n in range(NS):
            full[4 * n:4 * n + 4, n, :, :] = tn
        tabs[ch, 0:64, 520:648] = full.reshape(64, 128)
    c = np.zeros((128, 512), np.float32)
    c[:, 0:128] = np.eye(128)
    c[:, 128:192] = 1.0
    c[:, 320:384] = 1.0
    t = np.arange(16)
    for cc, ws in enumerate(((2, 4), (8, 16))):
        for hi in range(2):
            c[64 * hi:64 * hi + 64, 384 + 16 * cc:400 + 16 * cc] = 1.0 / np.minimum(t + 1, ws[hi])
    return tabs, c


_CACHE = {}


def kernel(x_prompt, x_sample, state_a, state_b1, state_b2, state_b3, state_c, norm_w, final_norm_w, w_in, sink_a,
           w_proj_a, w_proj_b, w_proj_c, w_pool, pool_scale, w_out):
    f = lambda a: np.ascontiguousarray(np.asarray(a, dtype=np.float32))
    DEPTH = w_in.shape[0]
    NS = x_sample.shape[0] // NCORES
    key = (DEPTH, NS)
    if key not in _CACHE:
        _CACHE[key] = build(DEPTH, NS)
    nc = _CACHE[key]
    tabs, consts = _host_consts(NS)
    x_prompt, x_sample = f(x_prompt), f(x_sample)
    sa, sb1, sb2, sb3, sc = f(state_a), f(state_b1), f(state_b2), f(state_b3), f(state_c)
    nwl = f(np.asarray(norm_w).reshape(DEPTH, 8, 128).transpose(0, 2, 1))
    fn = f(np.broadcast_to(np.asarray(final_norm_w)[None, :], (128, D)))
    sk = np.asarray(sink_a)
    sinkx = np.zeros((DEPTH, 128, 4), np.float32)
    for c in range(4):
        sinkx[:, 0:64, c] = sk[:, 2 * c][:, None]
        sinkx[:, 64:128, c] = sk[:, 2 * c + 1][:, None]
    psl = f(np.asarray(pool_scale).reshape(DEPTH, 2, 128).transpose(0, 2, 1))
    shared = dict(nw=nwl, fnw=fn, w_in=f(w_in), sinkx=sinkx, wpa=f(w_proj_a), wpb=f(w_proj_b), wpc=f(w_proj_c), wpool=f(w_pool),
                  pscale=psl, w_out=f(w_out), tabs=tabs, consts=consts)
    in_maps = []
    for c in range(NCORES):
        n0, n1 = c * NS, (c + 1) * NS
        m = dict(shared)
        m.update(xp=x_prompt[c], xs=x_sample[n0:n1].reshape(NS * 4, D),
                 sa=f(sa[:, n0:n1].reshape(DEPTH, NS, 2, 128, 128)),
                 sb1=f(sb1[:, n0:n1].reshape(DEPTH, NS, 2, 128, 256)),
                 sb2=f(sb2[:, n0:n1].reshape(DEPTH, NS, 2, 512, 256)),
                 sb3=f(sb3[:, n0:n1].reshape(DEPTH, NS, 2, 2048, 256)),
                 sc=f(sc[:, n0:n1]))
        in_maps.append(m)
    rr = run_bass_kernel_spmd(nc, in_maps, core_ids=list(range(NCORES))).results
    B = NCORES

    def cat(name, axis, shape):
        return np.concatenate([np.asarray(r[name], np.float32)[None] if axis is None else np.asarray(r[name], np.float32) for r in rr],
                              axis=0 if axis is None else axis).reshape(shape)
    y_p = np.stack([r["yp"] for r in rr]).astype(np.float32)
    y_s = np.concatenate([np.asarray(r["ys"]).reshape(NS, 4, D) for r in rr], 0).astype(np.float32)

    def pst_(name, tail):
        return np.stack([np.asarray(r[name]) for r in rr], 1).reshape((DEPTH, B) + tail).astype(np.float32)

    def sst_(name, tail):
        return np.concatenate([np.asarray(r[name]) for r in rr], 1).reshape((DEPTH, B * NS) + tail).astype(np.float32)
    return (y_p, y_s,
            pst_("nap", (2, 128, 2, 64)), sst_("nas", (2, 128, 2, 64)),
            pst_("nb1p", (2, 128, 4, 64)), sst_("nb1s", (2, 128, 4, 64)),
            pst_("nb2p", (2, 512, 4, 64)), sst_("nb2s", (2, 512, 4, 64)),
            pst_("nb3p", (2, 2048, 4, 64)), sst_("nb3s", (2, 2048, 4, 64)),
            pst_("ncp", (15, 256)), sst_("ncs", (15, 256)))
```
